# Optimizing a Trainium2 kernel written in Bass

```python
import math
import jax, jax.numpy as jnp
from jax import lax
import numpy as np

D_MODEL = 1024
BATCH = 8
SEQ = 4096
DEPTH = 2

HEAD_DIM = 64
N_HEADS_DIFF = 4
N_HEADS_FOX = 4
N_HEADS_SB = 4
N_HEADS_DSA = 4
N_IDX_HEADS = 8
IDX_DIM = 64
TOPK_MAX = 256
ROPE_THETA = 500000.0
ROPE_DIM = HEAD_DIM // 4
Q_BLOCK = 128
D_FF = 2816
N_BRANCH = 4
NORM_EPS = 1e-6
SUBLN_EPS = 1e-5

W_DIFF = N_HEADS_DIFF * 2 * HEAD_DIM
W_FOX = N_HEADS_FOX * HEAD_DIM
W_SB = N_HEADS_SB * HEAD_DIM
W_DSA = N_HEADS_DSA * HEAD_DIM
IN_SPLITS = (
    W_DIFF, W_DIFF, W_DIFF,
    W_FOX, W_FOX, W_FOX, N_HEADS_FOX,
    W_SB, W_SB, W_SB,
    W_DSA, W_DSA, W_DSA, N_IDX_HEADS * IDX_DIM, IDX_DIM, N_IDX_HEADS,
)
IN_WIDTH = sum(IN_SPLITS)

kernel_name = "hybrid_gated_four_mixer_decoder"


def rms_norm(x, g, eps=NORM_EPS):
    xf = x.astype(jnp.float32)
    y = xf * lax.rsqrt(jnp.mean(xf * xf, axis=-1, keepdims=True) + eps)
    return (y * g.astype(jnp.float32)).astype(x.dtype)


def swiglu(x, w_gu, w_down):
    g, u = jnp.split(x @ w_gu, 2, axis=-1)
    return (jax.nn.silu(g) * u) @ w_down


def rope_tables(positions):
    freqs = ROPE_THETA ** (-jnp.arange(0, ROPE_DIM, 2, dtype=jnp.float32) / ROPE_DIM)
    ang = positions.astype(jnp.float32)[:, None] * freqs[None, :]
    return jnp.cos(ang), jnp.sin(ang)


def partial_rope(x, cos, sin):
    shape = (1, cos.shape[0]) + (1,) * (x.ndim - 3) + (cos.shape[1],)
    c = cos.reshape(shape).astype(x.dtype)
    s = sin.reshape(shape).astype(x.dtype)
    half = ROPE_DIM // 2
    x1, x2, xp = x[..., :half], x[..., half:ROPE_DIM], x[..., ROPE_DIM:]
    return jnp.concatenate([x1 * c - x2 * s, x2 * c + x1 * s, xp], axis=-1)


def sweep_query_blocks(block_fn, seq_len):
    n_blocks = seq_len // Q_BLOCK
    out = lax.map(block_fn, jnp.arange(n_blocks, dtype=jnp.int32) * Q_BLOCK)
    nb, b, blk, h, e = out.shape
    return jnp.moveaxis(out, 0, 1).reshape(b, nb * blk, h, e)


def diff_attention(q, k, v, lam):
    S = q.shape[1]
    scale = HEAD_DIM ** -0.5
    kpos = jnp.arange(S)

    def block(start):
        qb = lax.dynamic_slice_in_dim(q, start, Q_BLOCK, axis=1)
        qpos = start + jnp.arange(Q_BLOCK)
        causal = kpos[None, :] <= qpos[:, None]
        logits = jnp.einsum('bqhcd,bkhcd->bchqk', qb, k).astype(jnp.float32) * scale
        p = jax.nn.softmax(jnp.where(causal, logits, -jnp.inf), axis=-1)
        p = p[:, 0] - lam * p[:, 1]
        return jnp.einsum('bhqk,bkhe->bqhe', p.astype(v.dtype), v)

    return sweep_query_blocks(block, S)


def forgetting_attention(q, k, v, f_logits):
    S = q.shape[1]
    scale = HEAD_DIM ** -0.5
    kpos = jnp.arange(S)
    cum = jnp.transpose(jnp.cumsum(jax.nn.log_sigmoid(f_logits.astype(jnp.float32)), axis=1), (0, 2, 1))

    def block(start):
        qb = lax.dynamic_slice_in_dim(q, start, Q_BLOCK, axis=1)
        cq = lax.dynamic_slice_in_dim(cum, start, Q_BLOCK, axis=2)
        qpos = start + jnp.arange(Q_BLOCK)
        causal = kpos[None, :] <= qpos[:, None]
        logits = (jnp.einsum('bqhd,bkhd->bhqk', qb, k).astype(jnp.float32) * scale
                  + cq[..., :, None] - cum[:, :, None, :])
        p = jax.nn.softmax(jnp.where(causal, logits, -jnp.inf), axis=-1)
        return jnp.einsum('bhqk,bkhd->bqhd', p.astype(v.dtype), v)

    return sweep_query_blocks(block, S)


def stick_breaking_attention(q, k, v):
    S = q.shape[1]
    scale = HEAD_DIM ** -0.5
    kpos = jnp.arange(S)

    def block(start):
        qb = lax.dynamic_slice_in_dim(q, start, Q_BLOCK, axis=1)
        qpos = start + jnp.arange(Q_BLOCK)
        strict = kpos[None, :] < qpos[:, None]
        z = jnp.einsum('bqhd,bkhd->bhqk', qb, k).astype(jnp.float32) * scale
        log_1m = jnp.where(strict, jax.nn.log_sigmoid(-z), 0.0)
        after = lax.cumsum(log_1m, axis=3, reverse=True) - log_1m
        a = jnp.where(strict, jnp.exp(jax.nn.log_sigmoid(z) + after), 0.0)
        return jnp.einsum('bhqk,bkhd->bqhd', a.astype(v.dtype), v)

    return sweep_query_blocks(block, S)


def indexed_sparse_attention(q, k, v, iq, ik, iw, topk):
    B, S = q.shape[0], q.shape[1]
    scale = HEAD_DIM ** -0.5
    kpos = jnp.arange(S)
    bidx = jnp.arange(B)[:, None, None]

    def block(start):
        qb = lax.dynamic_slice_in_dim(q, start, Q_BLOCK, axis=1)
        iqb = lax.dynamic_slice_in_dim(iq, start, Q_BLOCK, axis=1)
        iwb = lax.dynamic_slice_in_dim(iw, start, Q_BLOCK, axis=1)
        qpos = start + jnp.arange(Q_BLOCK)
        causal = kpos[None, :] <= qpos[:, None]
        rel = jax.nn.relu(jnp.einsum('bqhd,bkd->bqhk', iqb, ik).astype(jnp.float32))
        score = jnp.einsum('bqhk,bqh->bqk', rel, iwb.astype(jnp.float32))
        score = jnp.where(causal[None], score, -jnp.inf)
        _, idx = lax.top_k(score, topk)
        valid = idx <= qpos[None, :, None]
        k_sel = k[bidx, idx]
        v_sel = v[bidx, idx]
        logits = jnp.einsum('bqhd,bqjhd->bhqj', qb, k_sel).astype(jnp.float32) * scale
        p = jax.nn.softmax(jnp.where(valid[:, None], logits, -jnp.inf), axis=-1)
        return jnp.einsum('bhqj,bqjhd->bqhd', p.astype(v.dtype), v_sel)

    return sweep_query_blocks(block, S)


def hybrid_mixer(h, cos, sin, layer_idx, w_in, b_fgt, lam_q1, lam_k1, lam_q2, lam_k2, diff_gain,
                 w_gate, b_gate, w_br_a, w_br_b, w_br_c, w_br_d, w_out):
    B, S, D = h.shape
    points = np.cumsum(IN_SPLITS)[:-1].tolist()
    (aq, ak, av, bq, bk, bv, bf, cq, ck, cv,
     dq, dk, dv, diq, dik, diw) = jnp.split(h @ w_in, points, axis=-1)

    lam_init = 0.8 - 0.6 * math.exp(-0.3 * layer_idx)
    lam = (jnp.exp(jnp.sum(lam_q1.astype(jnp.float32) * lam_k1.astype(jnp.float32)))
           - jnp.exp(jnp.sum(lam_q2.astype(jnp.float32) * lam_k2.astype(jnp.float32))) + lam_init)
    qa = partial_rope(aq.reshape(B, S, N_HEADS_DIFF, 2, HEAD_DIM), cos, sin)
    ka = partial_rope(ak.reshape(B, S, N_HEADS_DIFF, 2, HEAD_DIM), cos, sin)
    ya = diff_attention(qa, ka, av.reshape(B, S, N_HEADS_DIFF, 2 * HEAD_DIM), lam)
    ya = (rms_norm(ya, diff_gain, SUBLN_EPS) * (1.0 - lam_init)).reshape(B, S, W_DIFF)

    yb = forgetting_attention(bq.reshape(B, S, N_HEADS_FOX, HEAD_DIM),
                              bk.reshape(B, S, N_HEADS_FOX, HEAD_DIM),
                              bv.reshape(B, S, N_HEADS_FOX, HEAD_DIM),
                              bf + b_fgt).reshape(B, S, W_FOX)

    yc = stick_breaking_attention(cq.reshape(B, S, N_HEADS_SB, HEAD_DIM),
                                  ck.reshape(B, S, N_HEADS_SB, HEAD_DIM),
                                  cv.reshape(B, S, N_HEADS_SB, HEAD_DIM)).reshape(B, S, W_SB)

    topk = min(TOPK_MAX, S // 4)
    qd = partial_rope(dq.reshape(B, S, N_HEADS_DSA, HEAD_DIM), cos, sin)
    kd = partial_rope(dk.reshape(B, S, N_HEADS_DSA, HEAD_DIM), cos, sin)
    iq = partial_rope(diq.reshape(B, S, N_IDX_HEADS, IDX_DIM), cos, sin)
    ik = partial_rope(dik, cos, sin)
    yd = indexed_sparse_attention(qd, kd, dv.reshape(B, S, N_HEADS_DSA, HEAD_DIM),
                                  iq, ik, diw, topk).reshape(B, S, W_DSA)

    g = jax.nn.sigmoid(h @ w_gate + b_gate).reshape(B, S, N_BRANCH, D)
    merged = (g[:, :, 0] * (ya @ w_br_a) + g[:, :, 1] * (yb @ w_br_b)
              + g[:, :, 2] * (yc @ w_br_c) + g[:, :, 3] * (yd @ w_br_d))
    return merged @ w_out


def setup_inputs(seed: int = 0) -> dict:
    key = jax.random.key(seed)
    ks = iter(jax.random.split(key, 32))
    f32 = jnp.float32

    def nrm(shape, scale):
        return jax.random.normal(next(ks), shape, f32) * scale

    def gain(shape):
        return 1.0 + 0.02 * jax.random.normal(next(ks), shape, f32)

    L, D, F = DEPTH, D_MODEL, D_FF
    return {
        "x": jax.random.normal(next(ks), (BATCH, SEQ, D), f32),
        "positions": jnp.arange(SEQ, dtype=jnp.int32),
        "ffn1_norm": gain((L, D)),
        "ffn1_w_gu": nrm((L, D, 2 * F), D ** -0.5),
        "ffn1_w_down": nrm((L, F, D), F ** -0.5),
        "mix_norm": gain((L, D)),
        "w_in": nrm((L, D, IN_WIDTH), D ** -0.5),
        "b_fgt": 2.0 + 0.5 * jax.random.normal(next(ks), (L, N_HEADS_FOX), f32),
        "lam_q1": nrm((L, HEAD_DIM), 0.1),
        "lam_k1": nrm((L, HEAD_DIM), 0.1),
        "lam_q2": nrm((L, HEAD_DIM), 0.1),
        "lam_k2": nrm((L, HEAD_DIM), 0.1),
        "diff_gain": gain((L, 2 * HEAD_DIM)),
        "w_gate": nrm((L, D, N_BRANCH * D), D ** -0.5),
        "b_gate": nrm((L, N_BRANCH * D), 0.01),
        "w_br_a": nrm((L, W_DIFF, D), W_DIFF ** -0.5),
        "w_br_b": nrm((L, W_FOX, D), W_FOX ** -0.5),
        "w_br_c": nrm((L, W_SB, D), W_SB ** -0.5),
        "w_br_d": nrm((L, W_DSA, D), W_DSA ** -0.5),
        "w_out": nrm((L, D, D), 0.5 * D ** -0.5),
        "ffn2_norm": gain((L, D)),
        "ffn2_w_gu": nrm((L, D, 2 * F), D ** -0.5),
        "ffn2_w_down": nrm((L, F, D), F ** -0.5),
        "final_norm": gain((D,)),
    }


def reference(x, positions, ffn1_norm, ffn1_w_gu, ffn1_w_down, mix_norm, w_in, b_fgt,
              lam_q1, lam_k1, lam_q2, lam_k2, diff_gain, w_gate, b_gate,
              w_br_a, w_br_b, w_br_c, w_br_d, w_out, ffn2_norm, ffn2_w_gu, ffn2_w_down,
              final_norm):
    cos, sin = rope_tables(positions)
    for l in range(DEPTH):
        x = x + 0.5 * swiglu(rms_norm(x, ffn1_norm[l]), ffn1_w_gu[l], ffn1_w_down[l])
        x = x + hybrid_mixer(rms_norm(x, mix_norm[l]), cos, sin, l, w_in[l], b_fgt[l],
                             lam_q1[l], lam_k1[l], lam_q2[l], lam_k2[l], diff_gain[l],
                             w_gate[l], b_gate[l], w_br_a[l], w_br_b[l], w_br_c[l], w_br_d[l],
                             w_out[l])
        x = x + 0.5 * swiglu(rms_norm(x, ffn2_norm[l]), ffn2_w_gu[l], ffn2_w_down[l])
    return rms_norm(x, final_norm)
```

```python
import numpy as np
import concourse.bass as bass
import concourse.mybir as mybir
from concourse.bass_utils import run_bass_kernel_spmd

F32 = mybir.dt.float32
BF16 = mybir.dt.bfloat16
I32 = mybir.dt.int32
AF = mybir.ActivationFunctionType
ALU = mybir.AluOpType

S = 4096
D = 1024
FF = 2816
NFC = FF // 128
DEPTH = 2
TT = 512
NTT = S // TT
NEG = -30000.0
IN_WIDTH = 4428


class Res:
    def __init__(self, name, multi=False):
        self.name = name
        self.w = []
        self.r = []
        self.sem = None
        self.semv = 0
        self.multi = multi


class Buf:
    def __init__(self, t, name, multi=False):
        self.t = t
        self.res = Res(name, multi)

    def __getitem__(self, idx):
        return self.t[idx]


class Prog:
    ENG = ("pe", "act", "dve", "pool", "sp")

    def __init__(self, nc):
        self.nc = nc
        self.q = {e: [] for e in self.ENG}
        self.sem = {e: nc.alloc_semaphore("sem_" + e) for e in ("pe", "act", "dve", "pool")}
        self.cnt = {e: 0 for e in self.sem}
        self.seen = {e: {} for e in self.ENG}
        self.nsem = 4
        self.out_events = []
        self.ninst = 0
        self.dma_res = []
        self.sem_pool = []

    def sb(self, name, shape, dt, multi=False):
        return Buf(self.nc.alloc_sbuf_tensor(name, list(shape), dt), name, multi)

    def ps(self, name, shape, dt=F32):
        return Buf(self.nc.alloc_psum_tensor(name, list(shape), dt), name)

    def dram(self, name, shape, dt, kind="Internal", multi=True):
        t = self.nc.dram_tensor(name, list(shape), dt, kind=kind)
        b = Buf(t.ap(), name, multi)
        return b

    def _deps(self, reads, writes, is_dma=False, eng=None):
        ev = []
        own = self.sem.get(eng)
        for b in reads:
            ev.extend(b.res.w)
        for b in writes:
            r = b.res
            if not (r.multi and is_dma):
                ev.extend(x for x in r.w if x[0] is not own)
            ev.extend(r.r)
        return ev

    def _wait(self, eng, events):
        seen = self.seen[eng]
        best = {}
        for (sem, val) in events:
            k = sem.num
            if seen.get(k, 0) >= val:
                continue
            if k not in best or best[k][1] < val:
                best[k] = (sem, val)
        for k, (sem, val) in best.items():
            self.q[eng].append(("w", sem, val))
            seen[k] = val

    def _record(self, ev, reads, writes, is_dma=False):
        for b in writes:
            r = b.res
            r.w = [ev]
            r.r = []
        for b in reads:
            b.res.r.append(ev)

    def op(self, eng, fn, reads=(), writes=()):
        self._wait(eng, self._deps(reads, writes, eng=eng))
        self.cnt[eng] += 1
        ev = (self.sem[eng], self.cnt[eng])
        self.q[eng].append(("i", fn, self.sem[eng], 1))
        self._record(ev, reads, writes)
        self.ninst += 1
        return ev

    def dma(self, eng, out_ap, in_ap, reads=(), writes=(), is_out=False, **kw):
        self._wait(eng, self._deps(reads, writes, is_dma=True))
        r = writes[0].res
        if r.sem is None:
            if self.sem_pool:
                r.sem, r.semv = self.sem_pool.pop()
            else:
                r.sem = self.nc.alloc_semaphore(f"dsem{self.nsem}")
                self.nsem += 1
            self.dma_res.append(r)
        r.semv += 16
        ev = (r.sem, r.semv)
        self.q[eng].append(("i", lambda e: e.dma_start(out=out_ap, in_=in_ap, **kw), r.sem, 16))
        self._record(ev, reads, writes, is_dma=True)
        if is_out:
            self.out_events.append(ev)
        self.ninst += 1
        return ev

    def barrier(self):
        evs = [(self.sem[e], self.cnt[e]) for e in self.sem if self.cnt[e] > 0]
        evs += [(r.sem, r.semv) for r in self.dma_res if not getattr(r, "no_barrier", False)]
        for eng in self.ENG:
            self._wait(eng, evs)

    def release(self, bufs):
        for b in bufs:
            r = b.res
            if r.sem is not None:
                self.sem_pool.append((r.sem, r.semv))
                self.dma_res.remove(r)
                r.sem = None

    def finish(self):
        nc = self.nc
        self._wait("sp", self.out_events)
        q = self.q

        def replay(lst, e):
            for it in lst:
                if it[0] == "w":
                    e.wait_ge(it[1], it[2])
                else:
                    it[1](e).then_inc(it[2], it[3])

        with nc.Block() as block:
            @block.tensor
            def _(e):
                replay(q["pe"], e)

            @block.scalar
            def _(e):
                replay(q["act"], e)

            @block.vector
            def _(e):
                replay(q["dve"], e)

            @block.gpsimd
            def _(e):
                replay(q["pool"], e)

            @block.sync
            def _(e):
                replay(q["sp"], e)


def _kc_tile(w):
    k, n = w.shape
    return np.ascontiguousarray(w.reshape(k // 128, 128, n).transpose(1, 0, 2))


def lay_wgu(w):
    t = _kc_tile(w)
    g = t[:, :, :FF].reshape(128, 8, NFC, 128)
    u = t[:, :, FF:].reshape(128, 8, NFC, 128)
    gu = np.concatenate([g, u], axis=3)
    return np.ascontiguousarray(gu.transpose(2, 0, 1, 3))


def lay_wd(w):
    return _kc_tile(w)


FM_GROUPS = ([(0 + 128 * i, 128, True) for i in range(4)] +
             [(512 + 128 * i, 128, True) for i in range(4)] +
             [(1536 + 128 * i, 128, False) for i in range(2)] +
             [(1792 + 128 * i, 128, False) for i in range(2)] +
             [(2308 + 128 * i, 128, False) for i in range(2)] +
             [(2564 + 128 * i, 128, False) for i in range(2)] +
             [(3076 + 128 * i, 128, True) for i in range(2)] +
             [(3332 + 128 * i, 128, True) for i in range(2)] +
             [(3844 + 128 * i, 128, True) for i in range(4)] +
             [(4356, 64, True)] +
             [(2304, 4, False)])
NG = len(FM_GROUPS)
G_AQ, G_AK, G_BQ, G_BK, G_CQ, G_CK, G_DQ, G_DK, G_IQ, G_IK, G_BF = 0, 4, 8, 10, 12, 14, 16, 18, 20, 24, 25
TM_COLS = [(1024, 512), (2048, 256), (2820, 256), (3588, 256), (4420, 8)]
NTM = 1288


def lay_wfm(w):
    t = _kc_tile(w)
    out = np.zeros((NG, 128, 8, 256), np.float32)
    for g, (c0, n, rot) in enumerate(FM_GROUPS):
        out[g, :, :, 0:n] = t[:, :, c0:c0 + n]
        if rot:
            for m in range(n // 64):
                b = c0 + m * 64
                o = 128 + m * 64
                out[g, :, :, o:o + 8] = t[:, :, b + 8:b + 16]
                out[g, :, :, o + 8:o + 16] = t[:, :, b:b + 8]
    return out


def lay_wtm(w):
    t = _kc_tile(w)
    return np.ascontiguousarray(np.concatenate([t[:, :, c0:c0 + n] for c0, n in TM_COLS], axis=2))


VEC_COLS = {}
NIT = 24


def build_vecs(inp):
    cols = []

    def add(name, arr):
        VEC_COLS[name] = (sum(c.shape[1] for c in cols), arr.shape[1])
        cols.append(np.ascontiguousarray(arr, dtype=np.float32))

    for l in range(DEPTH):
        for nm in ("ffn1_norm", "mix_norm", "ffn2_norm"):
            add(f"{nm}{l}", np.asarray(inp[nm])[l].reshape(8, 128).T)
        add(f"diff_gain{l}", np.asarray(inp["diff_gain"])[l].reshape(128, 1))
        bf = np.zeros((128, 1), np.float32)
        bf[0:4, 0] = np.asarray(inp["b_fgt"])[l]
        add(f"b_fgt{l}", bf)
    add("final_norm", np.asarray(inp["final_norm"]).reshape(8, 128).T)
    fr = np.zeros((128, 2), np.float32)
    freqs = (500000.0 ** (-np.arange(0, 16, 2, dtype=np.float32) / np.float32(16))).astype(np.float32)
    for m in range(2):
        for i in range(8):
            fr[m * 64 + i, 0] = freqs[i]
            fr[m * 64 + 8 + i, 0] = freqs[i]
            fr[m * 64 + i, 1] = -1.0
            fr[m * 64 + 8 + i, 1] = 1.0
    add("rope", fr)
    add("pw", np.tile((2.0 ** -(np.arange(NIT) + 1.0)).astype(np.float32)[None, :], (128, 1)))
    add("cntb", np.tile(((np.arange(32) + 1.0) * 128.0 - 512.0 + 0.5).astype(np.float32)[None, :], (128, 1)))
    return np.concatenate(cols, axis=1)


def host_consts():
    k = np.arange(128)[:, None]
    q = np.arange(512)[None, :]
    cmask = np.zeros((12, 128, 512), np.float32)
    for r in range(4):
        cmask[r] = np.where(128 * r + k <= q, 0.0, NEG)
        cmask[4 + r] = np.where(128 * r + k < q, 0.0, NEG)
        cmask[8 + r] = np.where(128 * r + k < q, 1.0, 0.0)
    j = np.arange(128)[:, None]
    s = np.arange(128)[None, :]
    cmat = np.zeros((4, 128, 128), np.float32)
    cmat[0] = np.eye(128)
    cmat[1] = np.where(j >= s, -8.0, 0.0)
    cmat[2] = -8.0
    cmat[3] = 1.0
    cqk = np.where(s <= j, 0.0, -1e30).astype(np.float32)
    return {"ident": np.eye(128, dtype=np.float32), "cmask": cmask, "cmat": cmat, "cqk": cqk}


def weight_specs():
    specs = []
    for l in range(DEPTH):
        specs.append((f"ffn1_wgu{l}", (NFC, 128, 8, 256)))
        specs.append((f"ffn1_wd{l}", (128, NFC, D)))
        specs.append((f"wfm{l}", (NG, 128, 8, 256)))
        specs.append((f"wtm{l}", (128, 8, NTM)))
        specs.append((f"wgate{l}", (128, 8, 4096)))
        specs.append((f"wbr{l}", (128, 10, D)))
        specs.append((f"wout{l}", (128, 8, D)))
        specs.append((f"bgate{l}", (1, 4096)))
        specs.append((f"ffn2_wgu{l}", (NFC, 128, 8, 256)))
        specs.append((f"ffn2_wd{l}", (128, NFC, D)))
    return specs


def host_weights(inp):
    out = {}
    A = lambda k: np.asarray(inp[k], dtype=np.float32)
    for l in range(DEPTH):
        for which in ("ffn1", "ffn2"):
            out[f"{which}_wgu{l}"] = lay_wgu(A(f"{which}_w_gu")[l])
            out[f"{which}_wd{l}"] = lay_wd(A(f"{which}_w_down")[l])
        out[f"wfm{l}"] = lay_wfm(A("w_in")[l])
        out[f"wtm{l}"] = lay_wtm(A("w_in")[l])
        out[f"wgate{l}"] = _kc_tile(A("w_gate")[l])
        out[f"wbr{l}"] = _kc_tile(np.concatenate([A("w_br_a")[l], A("w_br_b")[l], A("w_br_c")[l], A("w_br_d")[l]], axis=0))
        out[f"wout{l}"] = _kc_tile(A("w_out")[l])
        out[f"bgate{l}"] = np.ascontiguousarray(A("b_gate")[l].reshape(1, 4096))
    return out


class Ctx:
    pass


class Arena:
    def __init__(self, P, t, nbytes):
        self.P, self.t, self.nbytes, self.off = P, t, nbytes, 0
        self.bufs = []

    def reset(self):
        self.P.barrier()
        self.P.release(self.bufs)
        self.bufs = []
        self.off = 0

    def alloc(self, name, shape, dt, parts=128):
        esz = 4 if dt in (F32, I32) else 2
        n = int(np.prod(shape[1:]))
        nb = (n * esz + 63) // 64 * 64
        assert self.off + nb <= self.nbytes, (name, self.off, nb, self.nbytes)
        ap = self.t[0:shape[0], self.off // 2:(self.off + nb) // 2]
        if esz == 4:
            ap = ap.bitcast(dt)
        ap = ap[:, 0:n]
        if len(shape) == 3:
            ap = ap.rearrange("p (a b) -> p a b", a=shape[1])
        elif len(shape) == 4:
            ap = ap.rearrange("p (a b c) -> p a b c", a=shape[1], b=shape[2])
        self.off += nb
        b = Buf(ap, name)
        self.bufs.append(b)
        return b


def x_tile_ap(x, tt):
    return x[tt * TT:(tt + 1) * TT, :].rearrange("(tb p) d -> p tb d", p=128)


def rms_stats(P, c, xb, eps_col):
    for tb in range(4):
        P.op("act", lambda e, tb=tb: e.activation(
            out=c.xn[:, tb, :], in_=xb[:, tb, :], func=AF.Square, accum_out=c.ms[:, tb:tb + 1]),
            reads=[xb], writes=[c.xn, c.ms])
    P.op("act", lambda e: e.activation(out=c.ms[:, 4:8], in_=c.ms[:, 0:4], func=AF.Ln, bias=eps_col,
                                       scale=1.0 / D),
         reads=[c.ms, c.epsb], writes=[c.ms])
    P.op("act", lambda e: e.activation(out=c.ms[:, 8:12], in_=c.ms[:, 4:8], func=AF.Exp, scale=-0.5),
         reads=[c.ms], writes=[c.ms])


def transpose_to_fm(P, c, src, dstT, scale_col0=None):
    for kc in range(8):
        pT = c.bank[kc % 2]
        for tb in range(4):
            P.op("pe", lambda e, pT=pT, tb=tb, kc=kc: e.transpose(
                pT[:, tb * 128:(tb + 1) * 128], src[:, tb, kc * 128:(kc + 1) * 128], c.ident[:, :]),
                reads=[src, c.ident], writes=[pT])
        if scale_col0 is None:
            P.op("act", lambda e, pT=pT, kc=kc: e.activation(out=dstT[:, kc, :], in_=pT[:, :], func=AF.Copy),
                 reads=[pT], writes=[dstT])
        else:
            P.op("act", lambda e, pT=pT, kc=kc: e.activation(
                out=dstT[:, kc, :], in_=pT[:, :], func=AF.Copy,
                scale=c.vecs[:, scale_col0 + kc:scale_col0 + kc + 1]),
                reads=[pT, c.vecs], writes=[dstT])


def norm_transpose(P, c, xb, gcol0):
    rms_stats(P, c, xb, c.epsb[:, 0:1])
    for tb in range(4):
        P.op("dve", lambda e, tb=tb: e.tensor_scalar(
            out=c.xn[:, tb, :], in0=xb[:, tb, :], scalar1=c.ms[:, 8 + tb:9 + tb], scalar2=None,
            op0=ALU.mult), reads=[xb, c.ms], writes=[c.xn])
    transpose_to_fm(P, c, c.xn, c.hT, gcol0)


def alloc_stream_common(c, A):
    c.xt = [A.alloc(f"xt{i}", [128, 4, D], F32) for i in range(2)]
    c.xn = A.alloc("xn", [128, 4, D], F32)
    c.hT = A.alloc("hT", [128, 8, TT], BF16)
    c.wring = [A.alloc(f"wr{i}", [128, 2048], BF16) for i in range(3)]


def ffn_phase(P, c, l, which, x_src, x_dst):
    A = c.arena
    A.reset()
    alloc_stream_common(c, A)
    c.aT = A.alloc("aT", [128, NFC, TT], BF16)
    c.wd = A.alloc("wd", [128, NFC, D], BF16)
    c.sg = [A.alloc(f"sg{i}", [128, TT], F32) for i in range(2)]
    wgu = c.wbf[f"{which}_wgu{l}"]
    wdn = c.wbf[f"{which}_wd{l}"]
    gcol0, _ = VEC_COLS[f"{which}_norm{l}"]

    for j in range(2):
        P.dma("sp", c.wd[:, j * 11:(j + 1) * 11, :], wdn[:, j * 11:(j + 1) * 11, :],
              reads=[wdn], writes=[c.wd])

    def load_x(tt):
        xb = c.xt[tt % 2]
        P.dma("sp", xb[:, :, :], x_tile_ap(x_src, tt), reads=[x_src], writes=[xb])

    nw = NTT * NFC
    wstate = {"next": 0}

    def issue_w(upto):
        while wstate["next"] <= upto and wstate["next"] < nw:
            i = wstate["next"]
            fc = i % NFC
            slot = c.wring[i % 3]
            P.dma("sp", slot[:, :], wgu[fc].rearrange("p k c -> p (k c)"), reads=[wgu], writes=[slot])
            wstate["next"] += 1

    load_x(0)
    for tt in range(NTT):
        xb = c.xt[tt % 2]
        if tt + 1 < NTT:
            load_x(tt + 1)
        issue_w(tt * NFC + 1)
        norm_transpose(P, c, xb, gcol0)
        for fc in range(NFC):
            i = tt * NFC + fc
            issue_w(i + 2)
            slot = c.wring[i % 3]
            w3 = slot[:, :].rearrange("p (k c) -> p k c", k=8)
            pg = c.bank[2 + 2 * (fc % 2)]
            pu = c.bank[3 + 2 * (fc % 2)]
            for half, pb in ((0, pg), (1, pu)):
                for kc in range(8):
                    P.op("pe", lambda e, pb=pb, w3=w3, kc=kc, half=half: e.matmul(
                        pb[:, :], w3[:, kc, half * 128:(half + 1) * 128], c.hT[:, kc, :],
                        start=(kc == 0), stop=(kc == 7)),
                        reads=[slot, c.hT], writes=[pb])
            sg = c.sg[fc % 2]
            P.op("act", lambda e, sg=sg, pg=pg: e.activation(out=sg[:, :], in_=pg[:, :], func=AF.Silu),
                 reads=[pg], writes=[sg])
            P.op("dve", lambda e, sg=sg, pu=pu, fc=fc: e.tensor_tensor(
                out=c.aT[:, fc, :], in0=sg[:, :], in1=pu[:, :], op=ALU.mult),
                reads=[sg, pu], writes=[c.aT])
        n = 0
        for tb in range(4):
            for dh in range(2):
                po = c.bank[n % 2]
                n += 1
                for fc in range(NFC):
                    P.op("pe", lambda e, po=po, fc=fc, tb=tb, dh=dh: e.matmul(
                        po[:, :], c.aT[:, fc, tb * 128:(tb + 1) * 128],
                        c.wd[:, fc, dh * 512:(dh + 1) * 512],
                        start=(fc == 0), stop=(fc == NFC - 1)),
                        reads=[c.aT, c.wd], writes=[po])
                P.op("dve", lambda e, po=po, tb=tb, dh=dh, xb=xb: e.scalar_tensor_tensor(
                    out=xb[:, tb, dh * 512:(dh + 1) * 512], in0=po[:, :], scalar=0.5,
                    in1=xb[:, tb, dh * 512:(dh + 1) * 512], op0=ALU.mult, op1=ALU.add),
                    reads=[po, xb], writes=[xb])
        P.dma("sp", x_tile_ap(x_dst, tt), xb[:, :, :], reads=[xb], writes=[x_dst])


def final_phase(P, c, x_src, out):
    A = c.arena
    A.reset()
    alloc_stream_common(c, A)
    gfin = A.alloc("gfin", [128, D], F32)
    P.dma("sp", gfin[:, :], c.gfin_in[:].partition_broadcast(128), reads=[c.gfin_in], writes=[gfin])
    for tt in range(NTT):
        xb = c.xt[tt % 2]
        P.dma("sp", xb[:, :, :], x_tile_ap(x_src, tt), reads=[x_src], writes=[xb])
        rms_stats(P, c, xb, c.epsb[:, 0:1])
        for tb in range(4):
            P.op("dve", lambda e, tb=tb, xb=xb: e.scalar_tensor_tensor(
                out=xb[:, tb, :], in0=xb[:, tb, :], scalar=c.ms[:, 8 + tb:9 + tb], in1=gfin[:, :],
                op0=ALU.mult, op1=ALU.mult), reads=[xb, c.ms, gfin], writes=[xb])
        P.dma("sp", x_tile_ap(out, tt), xb[:, :, :], reads=[xb], writes=[out], is_out=True)


def rope_tables(P, c):
    A = c.arena
    A.reset()
    posi = A.alloc("posi", [128, S], I32)
    ang = A.alloc("ang", [128, S], F32)
    t1 = A.alloc("t1", [128, S], F32)
    ki = A.alloc("ki", [128, S], I32)
    t2 = A.alloc("t2", [128, S], F32)
    rc0, _ = VEC_COLS["rope"]
    PI = float(np.pi)
    TWO_PI = float(2 * np.pi)
    P.dma("sp", posi[:, :], c.pos_in[:].partition_broadcast(128), reads=[c.pos_in], writes=[posi])
    P.op("dve", lambda e: e.tensor_copy(out=ang[:, :], in_=posi[:, :]), reads=[posi], writes=[ang])
    P.op("dve", lambda e: e.tensor_scalar(out=ang[:, :], in0=ang[:, :], scalar1=c.vecs[:, rc0:rc0 + 1],
                                          scalar2=None, op0=ALU.mult), reads=[ang, c.vecs], writes=[ang])
    for which, shift, dst in (("s", 0.0, c.ropeS_s), ("c", PI / 2, c.ropeC_s)):
        P.op("dve", lambda e, shift=shift: e.tensor_scalar(out=t1[:, :], in0=ang[:, :], scalar1=shift, scalar2=None,
                                                           op0=ALU.add), reads=[ang], writes=[t1])
        P.op("dve", lambda e: e.tensor_scalar(out=ki[:, :], in0=t1[:, :], scalar1=1.0 / TWO_PI, scalar2=None,
                                              op0=ALU.mult), reads=[t1], writes=[ki])
        P.op("dve", lambda e: e.tensor_copy(out=t2[:, :], in_=ki[:, :]), reads=[ki], writes=[t2])
        P.op("dve", lambda e: e.scalar_tensor_tensor(out=t1[:, :], in0=t2[:, :], scalar=-TWO_PI, in1=t1[:, :],
                                                     op0=ALU.mult, op1=ALU.add), reads=[t2, t1], writes=[t1])
        P.op("dve", lambda e: e.tensor_scalar(out=t2[:, :], in0=t1[:, :], scalar1=PI, scalar2=-TWO_PI,
                                              op0=ALU.is_gt, op1=ALU.mult), reads=[t1], writes=[t2])
        P.op("dve", lambda e: e.tensor_tensor(out=t1[:, :], in0=t1[:, :], in1=t2[:, :], op=ALU.add),
             reads=[t1, t2], writes=[t1])
        P.op("dve", lambda e: e.tensor_scalar(out=t2[:, :], in0=t1[:, :], scalar1=-PI, scalar2=TWO_PI,
                                              op0=ALU.is_lt, op1=ALU.mult), reads=[t1], writes=[t2])
        P.op("dve", lambda e: e.tensor_tensor(out=t1[:, :], in0=t1[:, :], in1=t2[:, :], op=ALU.add),
             reads=[t1, t2], writes=[t1])
        P.op("dve", lambda e: e.tensor_scalar(out=t1[:, :], in0=t1[:, :], scalar1=PI, scalar2=-PI,
                                              op0=ALU.min, op1=ALU.max), reads=[t1], writes=[t1])
        P.op("act", lambda e: e.activation(out=t2[:, :], in_=t1[:, :], func=AF.Sin),
             reads=[t1], writes=[t2])
        if which == "s":
            P.op("dve", lambda e: e.tensor_scalar(out=t2[:, :], in0=t2[:, :], scalar1=c.vecs[:, rc0 + 1:rc0 + 2],
                                                  scalar2=None, op0=ALU.mult), reads=[t2, c.vecs], writes=[t2])
        P.dma("sp", dst[:, :], t2[:, :], reads=[t2], writes=[dst])


def mix_proj_phase(P, c, l, x_src):
    A = c.arena
    A.reset()
    alloc_stream_common(c, A)
    wtm_sb = A.alloc("wtm", [128, 8, NTM], BF16)
    fst = [A.alloc(f"fst{i}", [128, TT], BF16) for i in range(3)]
    f32st = A.alloc("f32st", [128, TT], F32)
    r1 = [A.alloc(f"r1_{i}", [128, TT], F32) for i in range(2)]
    r2 = [A.alloc(f"r2_{i}", [128, TT], F32) for i in range(2)]
    vst = A.alloc("vst", [128, 4, 1280], BF16)
    iwst = A.alloc("iwst", [128, 4, 8], F32)
    rcs = [A.alloc(f"rc{i}", [128, TT], F32) for i in range(2)]
    rss = [A.alloc(f"rs{i}", [128, TT], F32) for i in range(2)]
    wfm = c.wbf[f"wfm{l}"]
    wtm = c.wbf[f"wtm{l}"]
    gcol0, _ = VEC_COLS[f"mix_norm{l}"]
    P.dma("sp", wtm_sb[:, :, :], wtm[:, :, :], reads=[wtm], writes=[wtm_sb])

    def load_x(tt):
        xb = c.xt[tt % 2]
        P.dma("sp", xb[:, :, :], x_tile_ap(x_src, tt), reads=[x_src], writes=[xb])

    nw = NTT * NG
    wstate = {"next": 0}

    def issue_w(upto):
        while wstate["next"] <= upto and wstate["next"] < nw:
            i = wstate["next"]
            slot = c.wring[i % 3]
            P.dma("sp", slot[:, :], wfm[i % NG].rearrange("p k c -> p (k c)"), reads=[wfm], writes=[slot])
            wstate["next"] += 1

    load_x(0)
    nst = 0
    for tt in range(NTT):
        xb = c.xt[tt % 2]
        tok = slice(tt * TT, (tt + 1) * TT)
        if tt + 1 < NTT:
            load_x(tt + 1)
        issue_w(tt * NG + 1)
        rc_, rs_ = rcs[tt % 2], rss[tt % 2]
        P.dma("sp", rc_[:, :], c.ropeC_s[:, tok], reads=[c.ropeC_s], writes=[rc_])
        P.dma("sp", rs_[:, :], c.ropeS_s[:, tok], reads=[c.ropeS_s], writes=[rs_])
        norm_transpose(P, c, xb, gcol0)
        P.dma("sp", c.hT_s[tt], c.hT[:, :, :], reads=[c.hT], writes=[c.hT_s])
        for g in range(NG):
            i = tt * NG + g
            issue_w(i + 2)
            slot = c.wring[i % 3]
            w3 = slot[:, :].rearrange("p (k c) -> p k c", k=8)
            rot = FM_GROUPS[g][2]
            pm = c.bank[2 + 2 * (g % 2)]
            pr = c.bank[3 + 2 * (g % 2)]
            for half, pb in (((0, pm), (1, pr)) if rot else ((0, pm),)):
                for kc in range(8):
                    P.op("pe", lambda e, pb=pb, w3=w3, kc=kc, half=half: e.matmul(
                        pb[:, :], w3[:, kc, half * 128:(half + 1) * 128], c.hT[:, kc, :],
                        start=(kc == 0), stop=(kc == 7)),
                        reads=[slot, c.hT], writes=[pb])
            if g == G_BF:
                P.op("act", lambda e, pm=pm: e.activation(out=f32st[:, :], in_=pm[:, :], func=AF.Copy),
                     reads=[pm], writes=[f32st])
                P.dma("sp", c.fT_s[0:4, tok], f32st[0:4, :], reads=[f32st], writes=[c.fT_s])
                continue
            st = fst[nst % 3]
            nst += 1
            if rot:
                a1, a2 = r1[g % 2], r2[g % 2]
                P.op("dve", lambda e, pm=pm, a1=a1, rc_=rc_: e.tensor_tensor(
                    out=a1[:, :], in0=pm[:, :], in1=rc_[:, :], op=ALU.mult),
                    reads=[pm, rc_], writes=[a1])
                P.op("dve", lambda e, pr=pr, a2=a2, rs_=rs_: e.tensor_tensor(
                    out=a2[:, :], in0=pr[:, :], in1=rs_[:, :], op=ALU.mult),
                    reads=[pr, rs_], writes=[a2])
                P.op("pool", lambda e, a1=a1, a2=a2, st=st: e.tensor_tensor(
                    out=st[:, :], in0=a1[:, :], in1=a2[:, :], op=ALU.add),
                    reads=[a1, a2], writes=[st])
            else:
                P.op("act", lambda e, pm=pm, st=st: e.activation(out=st[:, :], in_=pm[:, :], func=AF.Copy),
                     reads=[pm], writes=[st])
            P.dma("sp", c.fmq[g][:, tok], st[:, :], reads=[st], writes=[c.fmq])
        nb = 0
        for tb in range(4):
            for (c0, n) in ((0, 512), (512, 512), (1024, 264)):
                pb = c.bank[nb % 2]
                nb += 1
                for kc in range(8):
                    P.op("pe", lambda e, pb=pb, kc=kc, tb=tb, c0=c0, n=n: e.matmul(
                        pb[:, 0:n], c.hT[:, kc, tb * 128:(tb + 1) * 128], wtm_sb[:, kc, c0:c0 + n],
                        start=(kc == 0), stop=(kc == 7)),
                        reads=[c.hT, wtm_sb], writes=[pb])
                nv = min(n, 1280 - c0)
                P.op("act", lambda e, pb=pb, tb=tb, c0=c0, nv=nv: e.activation(
                    out=vst[:, tb, c0:c0 + nv], in_=pb[:, 0:nv], func=AF.Copy),
                    reads=[pb], writes=[vst])
                if c0 == 1024:
                    P.op("act", lambda e, pb=pb, tb=tb: e.activation(
                        out=iwst[:, tb, :], in_=pb[:, 256:264], func=AF.Copy),
                        reads=[pb], writes=[iwst])
        P.dma("sp", c.vtm[tok, :].rearrange("(tb p) c -> p tb c", p=128), vst[:, :, :],
              reads=[vst], writes=[c.vtm])
        P.dma("sp", c.iw_s[tok, :].rearrange("(tb p) c -> p tb c", p=128), iwst[:, :, :],
              reads=[iwst], writes=[c.iw_s])


class Indexer:
    def __init__(self, P, c):
        self.P, self.c = P, c
        self.next = 0
        self.t_slot = 0.0
        self.t_job = 0.0
        self.n = {"s": 0, "rl": 0}
        jt = [self.job_cost(q) for q in range(32)]
        self.jt = jt
        tot_slots = sum(4 * (4 * j + 4) * 1.8 for j in range(8))
        self.scale = sum(jt) / tot_slots

    @staticmethod
    def job_cost(qb):
        j = qb // 4
        return 8 * (j + 1) * 0.6 + (NIT * (0.133 * (qb + 1) + 0.3) if qb >= 2 else 0.0)

    def setup(self, A):
        P, c = self.P, self.c
        self.IKz = [A.alloc(f"ikz{m}", [128, S], BF16) for m in range(2)]
        self.iq = [A.alloc(f"iqb{i}", [128, 4, 128], BF16) for i in range(2)]
        self.sc = A.alloc("isc", [128, S], F32)
        self.mb = [A.alloc(f"imb{i}", [128, S], BF16) for i in range(2)]
        self.rl = [A.alloc(f"irl{i}", [128, 512], F32) for i in range(2)]
        self.iw = A.alloc("iiw", [128, 32, 8], F32)
        self.bs = A.alloc("ibs", [128, 8 + NIT], F32)
        for m in range(2):
            P.op("pool", lambda e, m=m: e.memset(self.IKz[m][:, :], 0.0), writes=[self.IKz[m]])
            P.dma("sp", self.IKz[m][m * 64:(m + 1) * 64, :], c.fmq[G_IK][0:64, :], reads=[c.fmq], writes=[self.IKz[m]])
        P.dma("sp", self.iw[:, :, :], c.iw_s[:, :].rearrange("(qb p) c -> p qb c", p=128), reads=[c.iw_s], writes=[self.iw])

    def slot(self, cost_us, flush=False):
        self.t_slot += cost_us
        while self.next < 32 and (flush or self.t_job + self.jt[self.next] * 0.5 <= self.t_slot * self.scale):
            self.job(self.next)
            self.t_job += self.jt[self.next]
            self.next += 1

    def job(self, qb):
        P, c = self.P, self.c
        j, sub = divmod(qb, 4)
        nval = (qb + 1) * 128
        iq, mb, sc, bs, iw = self.iq[qb % 2], self.mb[qb % 2], self.sc, self.bs, self.iw
        pw0, _ = VEC_COLS["pw"]
        for g in range(4):
            P.dma("sp", iq[:, g, :], c.fmq[G_IQ + g][:, qb * 128:(qb + 1) * 128], reads=[c.fmq], writes=[iq])
        for kc in range(j + 1):
            w = 512 if kc < j else (sub + 1) * 128
            ks = slice(kc * 512, kc * 512 + w)
            for hi_ in range(8):
                ps = c.bank[self.n["s"] % 3]
                self.n["s"] += 1
                r = self.rl[self.n["rl"] % 2]
                self.n["rl"] += 1
                IKm = self.IKz[hi_ % 2]
                P.op("pe", lambda e, ps=ps, hi_=hi_, ks=ks, w=w, IKm=IKm: e.matmul(
                    ps[:, 0:w], iq[:, hi_ // 2, :], IKm[:, ks], start=True, stop=True),
                    reads=[iq, IKm], writes=[ps])
                P.op("act", lambda e, ps=ps, r=r, w=w: e.activation(out=r[:, 0:w], in_=ps[:, 0:w], func=AF.Relu),
                     reads=[ps], writes=[r])
                if hi_ == 0:
                    P.op("dve", lambda e, r=r, ks=ks, w=w: e.tensor_scalar(
                        out=sc[:, ks], in0=r[:, 0:w], scalar1=iw[:, qb, 0:1], scalar2=None, op0=ALU.mult),
                        reads=[r, iw], writes=[sc])
                else:
                    P.op("dve", lambda e, r=r, ks=ks, w=w, hi_=hi_: e.scalar_tensor_tensor(
                        out=sc[:, ks], in0=r[:, 0:w], scalar=iw[:, qb, hi_:hi_ + 1], in1=sc[:, ks],
                        op0=ALU.mult, op1=ALU.add), reads=[r, iw, sc], writes=[sc])
        if qb >= 2:
            P.op("dve", lambda e: e.tensor_reduce(out=bs[:, 0:1], in_=sc[:, 0:nval], axis=mybir.AxisListType.X,
                                                  op=ALU.max), reads=[sc], writes=[bs])
            P.op("dve", lambda e: e.tensor_reduce(out=bs[:, 1:2], in_=sc[:, 0:nval], axis=mybir.AxisListType.X,
                                                  op=ALU.min), reads=[sc], writes=[bs])
        P.op("dve", lambda e: e.tensor_tensor(out=sc[:, qb * 128:(qb + 1) * 128], in0=sc[:, qb * 128:(qb + 1) * 128],
                                              in1=c.cqk[:, :], op=ALU.add), reads=[sc, c.cqk], writes=[sc])
        if qb >= 2:
            P.op("dve", lambda e: e.tensor_tensor(out=bs[:, 2:3], in0=bs[:, 0:1], in1=bs[:, 1:2], op=ALU.subtract),
                 reads=[bs], writes=[bs])
            P.op("dve", lambda e: e.tensor_scalar(out=bs[:, 8:8 + NIT], in0=c.vecs[:, pw0:pw0 + NIT],
                                                  scalar1=bs[:, 2:3], scalar2=None, op0=ALU.mult),
                 reads=[bs, c.vecs], writes=[bs])
            P.op("dve", lambda e: e.scalar_tensor_tensor(out=bs[:, 3:4], in0=bs[:, 2:3], scalar=0.5, in1=bs[:, 1:2],
                                                         op0=ALU.mult, op1=ALU.add), reads=[bs], writes=[bs])
            for it in range(NIT):
                P.op("dve", lambda e: e.tensor_scalar(
                    out=mb[:, 0:nval], in0=sc[:, 0:nval], scalar1=bs[:, 3:4], scalar2=None,
                    op0=ALU.is_ge, op1=ALU.add, accum_out=bs[:, 4:5]), reads=[sc, bs], writes=[mb, bs])
                P.op("dve", lambda e: e.tensor_scalar(out=bs[:, 5:6], in0=bs[:, 4:5], scalar1=256.0, scalar2=-0.5,
                                                      op0=ALU.is_ge, op1=ALU.add), reads=[bs], writes=[bs])
                if it < NIT - 1:
                    P.op("dve", lambda e, it=it: e.scalar_tensor_tensor(
                        out=bs[:, 3:4], in0=bs[:, 5:6], scalar=bs[:, 8 + it:9 + it], in1=bs[:, 3:4],
                        op0=ALU.mult, op1=ALU.add), reads=[bs], writes=[bs])
            P.op("dve", lambda e: e.tensor_scalar(out=bs[:, 5:6], in0=bs[:, 5:6], scalar1=-0.5, scalar2=None,
                                                  op0=ALU.add), reads=[bs], writes=[bs])
            P.op("dve", lambda e: e.scalar_tensor_tensor(
                out=bs[:, 6:7], in0=bs[:, 5:6], scalar=bs[:, 8 + NIT - 1:8 + NIT], in1=bs[:, 3:4],
                op0=ALU.mult, op1=ALU.add), reads=[bs], writes=[bs])
        else:
            P.op("dve", lambda e: e.memset(bs[:, 6:7], -1e29), writes=[bs])
        P.op("dve", lambda e: e.tensor_scalar(
            out=mb[:, 0:nval], in0=sc[:, 0:nval], scalar1=bs[:, 6:7], scalar2=NEG,
            op0=ALU.is_lt, op1=ALU.mult), reads=[sc, bs], writes=[mb])
        P.dma("sp", c.maskq[qb][:, 0:nval], mb[:, 0:nval], reads=[mb], writes=[c.maskq])


def skew_emit(stage_lists, lag):
    ns = len(stage_lists)
    n = len(stage_lists[0])
    for step in range(n + (ns - 1) * lag):
        for k in range(ns):
            t = step - k * lag
            if 0 <= t < n:
                stage_lists[k][t]()


def attn_A(P, c, l):
    A = c.arena
    A.reset()
    lam_init = 0.8 - 0.6 * float(np.exp(-0.3 * l))
    Qz = [A.alloc(f"aqz{b}", [128, 8, 512], BF16) for b in range(2)]
    K = [A.alloc(f"ak{h}", [128, S], BF16) for h in range(4)]
    V = A.alloc("av", [128, 32, 512], BF16)
    pt = [A.alloc(f"pt{i}", [128, 512], BF16) for i in range(4)]
    bcs = [A.alloc(f"bcs{i}", [128, 512], F32) for i in range(2)]
    tt_ = [A.alloc(f"tA{i}", [128, 512], F32) for i in range(2)]
    yb = A.alloc("yA", [128, 512], F32)
    sq = A.alloc("sqA", [128, 512], F32)
    rs = A.alloc("rsA", [128, 512], F32)
    st = [A.alloc(f"stA{i}", [128, 512], BF16) for i in range(2)]
    lamv = A.alloc("lamv", [128, 4, 64], F32)
    lamt = A.alloc("lamt", [128, 2, 64], F32)
    lams = A.alloc("lams", [128, 8], F32)
    for b in range(2):
        P.op("pool", lambda e, b=b: e.memset(Qz[b][:, :, :], 0.0), writes=[Qz[b]])

    def load_q(j):
        qs = slice(j * 512, (j + 1) * 512)
        for h in range(4):
            for m in range(2):
                rows = slice(m * 64, (m + 1) * 64)
                P.dma("sp", Qz[j % 2][rows, h * 2 + m, :], c.fmq[G_AQ + h][rows, qs], reads=[c.fmq], writes=[Qz[j % 2]])

    load_q(0)
    for h in range(4):
        P.dma("sp", K[h][:, :], c.fmq[G_AK + h], reads=[c.fmq], writes=[K[h]])
    P.dma("sp", V[:, :, :], c.vtm[:, 0:512].rearrange("(kb p) c -> p kb c", p=128), reads=[c.vtm], writes=[V])
    for i, nm in enumerate(("lam_q1", "lam_k1", "lam_q2", "lam_k2")):
        P.dma("sp", lamv[:, i, :], c.lam_in[nm][l, :].partition_broadcast(128), reads=[c.lam_in[nm]], writes=[lamv])
    for i in range(2):
        P.op("dve", lambda e, i=i: e.tensor_tensor(out=lamt[:, i, :], in0=lamv[:, 2 * i, :], in1=lamv[:, 2 * i + 1, :],
                                                   op=ALU.mult), reads=[lamv], writes=[lamt])
        P.op("dve", lambda e, i=i: e.tensor_scalar(out=lamv[:, i, :], in0=lamt[:, i, :], scalar1=1.0, scalar2=None,
                                                   op0=ALU.mult, op1=ALU.add, accum_out=lams[:, i:i + 1]),
             reads=[lamt], writes=[lamv, lams])
    P.op("act", lambda e: e.activation(out=lams[:, 2:4], in_=lams[:, 0:2], func=AF.Exp), reads=[lams], writes=[lams])
    P.op("dve", lambda e: e.tensor_scalar(out=lams[:, 4:5], in0=lams[:, 3:4], scalar1=lams[:, 2:3], scalar2=-lam_init,
                                          op0=ALU.subtract, op1=ALU.add), reads=[lams], writes=[lams])
    gc0, _ = VEC_COLS[f"diff_gain{l}"]
    P.op("dve", lambda e: e.tensor_scalar(out=lams[:, 5:6], in0=c.vecs[:, gc0:gc0 + 1], scalar1=1.0 - lam_init,
                                          scalar2=None, op0=ALU.mult), reads=[c.vecs], writes=[lams])
    cnt = {"s": 0, "p": 0}
    O = [c.bank[3], c.bank[4]]
    den = [c.bank[5], c.bank[6]]
    epi = c.bank7
    for j in range(8):
        qs = slice(j * 512, (j + 1) * 512)
        if j + 1 < 8:
            load_q(j + 1)
        Qb = Qz[j % 2]
        for h in range(4):
            nkb = 4 * j + 4
            s1, s2 = [], []
            for kb in range(nkb):
                for m in range(2):
                    ps = c.bank[cnt["s"] % 3]
                    cnt["s"] += 1
                    p_ = pt[cnt["p"] % 4]
                    cnt["p"] += 1

                    def f1(ps=ps, p_=p_, kb=kb, m=m, h=h, j=j, Qb=Qb):
                        diag = kb >= 4 * j
                        P.op("pe", lambda e: e.matmul(ps[:, :], K[h][:, kb * 128:(kb + 1) * 128], Qb[:, h * 2 + m, :],
                                                      start=True, stop=not diag), reads=[K[h], Qb], writes=[ps])
                        if diag:
                            r = kb - 4 * j
                            P.op("pe", lambda e: e.matmul(ps[:, :], c.identb[:, :], c.cmask[:, r, :], start=False, stop=True),
                                 reads=[c.identb, c.cmask], writes=[ps])
                        P.op("act", lambda e: e.activation(out=p_[:, :], in_=ps[:, :], func=AF.Exp, scale=0.125),
                             reads=[ps], writes=[p_])

                    def f2(p_=p_, kb=kb, m=m, h=h, nkb=nkb):
                        P.op("pe", lambda e: e.matmul(O[m][:, :], V[:, kb, h * 128:(h + 1) * 128], p_[:, :],
                                                      start=(kb == 0), stop=(kb == nkb - 1)), reads=[V, p_], writes=[O[m]])
                        P.op("pe", lambda e: e.matmul(den[m][:, :], c.onesb[:, :], p_[:, :],
                                                      start=(kb == 0), stop=(kb == nkb - 1)), reads=[c.onesb, p_], writes=[den[m]])
                    s1.append(f1)
                    s2.append(f2)
            skew_emit([s1, s2], 2)
            for m in range(2):
                P.op("act", lambda e, m=m: e.activation(out=bcs[m][:, :], in_=den[m][:, :], func=AF.Ln),
                     reads=[den[m]], writes=[bcs[m]])
                P.op("act", lambda e, m=m: e.activation(out=bcs[m][:, :], in_=bcs[m][:, :], func=AF.Exp, scale=-1.0),
                     reads=[bcs[m]], writes=[bcs[m]])
                P.op("dve", lambda e, m=m: e.tensor_tensor(out=tt_[m][:, :], in0=O[m][:, :], in1=bcs[m][:, :],
                                                           op=ALU.mult), reads=[O[m], bcs[m]], writes=[tt_[m]])
            P.op("dve", lambda e: e.scalar_tensor_tensor(out=yb[:, :], in0=tt_[1][:, :], scalar=lams[:, 4:5],
                                                         in1=tt_[0][:, :], op0=ALU.mult, op1=ALU.add),
                 reads=[tt_[0], tt_[1], lams], writes=[yb])
            P.op("act", lambda e: e.activation(out=sq[:, :], in_=yb[:, :], func=AF.Square), reads=[yb], writes=[sq])
            P.op("pe", lambda e: e.matmul(epi[:, :], c.ones32[:, :], sq[:, :], start=True, stop=True),
                 reads=[c.ones32, sq], writes=[epi])
            P.op("act", lambda e: e.activation(out=rs[:, :], in_=epi[:, :], func=AF.Ln, bias=c.epsb[:, 1:2],
                                               scale=1.0 / 128), reads=[epi, c.epsb], writes=[rs])
            P.op("act", lambda e: e.activation(out=rs[:, :], in_=rs[:, :], func=AF.Exp, scale=-0.5),
                 reads=[rs], writes=[rs])
            s_ = st[(j * 4 + h) % 2]
            P.op("dve", lambda e, s_=s_: e.scalar_tensor_tensor(out=s_[:, :], in0=yb[:, :], scalar=lams[:, 5:6],
                                                                in1=rs[:, :], op0=ALU.mult, op1=ALU.mult),
                 reads=[yb, rs, lams], writes=[s_])
            P.dma("sp", c.yT[h][:, qs], s_[:, :], reads=[s_], writes=[c.yT])


def attn_BD(P, c, l, kind):
    A = c.arena
    A.reset()
    isB = kind == "B"
    pt = [A.alloc(f"pt{i}", [128, 512], BF16) for i in range(4)]
    bcs = A.alloc("bcs", [128, 512], F32)
    st = [A.alloc(f"st{i}", [128, 512], BF16) for i in range(2)]
    V = A.alloc("v", [128, 32, 4, 65], BF16)
    P.op("pool", lambda e: e.memset(V[:, :, :, :], 1.0), writes=[V])
    vcol = 512 if isB else 1024
    for h in range(4):
        src = c.vtm[:, vcol + h * 64:vcol + (h + 1) * 64].rearrange("(kb p) c -> p kb c", p=128)
        P.dma("sp", V[:, :, h, 0:64], src, reads=[c.vtm], writes=[V])
    if isB:
        Q = [A.alloc(f"bq{h}", [128, S], BF16) for h in range(4)]
        K = [A.alloc(f"bk{h}", [128, S], BF16) for h in range(4)]
        for h in range(4):
            hr = slice((h % 2) * 64, (h % 2) * 64 + 64)
            P.op("pool", lambda e, h=h: e.memset(Q[h][:, :], 0.0), writes=[Q[h]])
            P.op("pool", lambda e, h=h: e.memset(K[h][:, :], 0.0), writes=[K[h]])
            P.dma("sp", Q[h][0:64, :], c.fmq[G_BQ + h // 2][hr, :], reads=[c.fmq], writes=[Q[h]])
            P.dma("sp", K[h][0:64, :], c.fmq[G_BK + h // 2][hr, :], reads=[c.fmq], writes=[K[h]])
            P.dma("sp", Q[h][64:65, :], c.caug_s[h:h + 1, :], reads=[c.caug_s], writes=[Q[h]])
            P.dma("sp", Q[h][65:66, :], c.caug_s[4 + h:5 + h, :], reads=[c.caug_s], writes=[Q[h]])
            P.op("pool", lambda e, h=h: e.memset(K[h][64:66, :], 1.0), writes=[K[h]])
        chunk_y0 = 4

        def load_q(j):
            pass
    else:
        Kp = [A.alloc(f"dk{i}", [128, S], BF16) for i in range(2)]
        for i in range(2):
            P.dma("sp", Kp[i][:, :], c.fmq[G_DK + i], reads=[c.fmq], writes=[Kp[i]])
        chunk_y0 = 8
        Qz = [A.alloc(f"dqz{b}", [128, 4, 512], BF16) for b in range(2)]
        for b in range(2):
            P.op("pool", lambda e, b=b: e.memset(Qz[b][:, :, :], 0.0), writes=[Qz[b]])
        mbs = [A.alloc(f"mb{i}", [128, S], BF16) for i in range(4)]
        maskT = A.alloc("maskT", [128, 32, 512], BF16)

        def load_q(j):
            qs = slice(j * 512, (j + 1) * 512)
            for h in range(4):
                rows = slice((h % 2) * 64, (h % 2) * 64 + 64)
                P.dma("sp", Qz[j % 2][rows, h, :], c.fmq[G_DQ + h // 2][rows, qs], reads=[c.fmq], writes=[Qz[j % 2]])

        def load_mb(j):
            for sub in range(4):
                qb = 4 * j + sub
                nval = (qb + 1) * 128
                P.dma("sp", mbs[sub][:, 0:nval], c.maskq[qb][:, 0:nval], reads=[c.maskq], writes=[mbs[sub]])
    cnt = {"s": 0, "p": 0, "acc": 0}
    epi = c.bank[6]

    def transposes(j):
        for r_ in range(1, 4):
            kb = 4 * j + r_
            P.op("pool", lambda e, kb=kb, r_=r_: e.memset(maskT[:, kb, 0:128 * r_], NEG), writes=[maskT])
        for sub in range(4):
            qb = 4 * j + sub
            mb = mbs[sub]
            for kb0 in range(0, qb + 1, 4):
                nkk = min(4, qb + 1 - kb0)
                for kk in range(nkk):
                    kb = kb0 + kk
                    P.op("pe", lambda e, kb=kb, kk=kk, mb=mb: e.transpose(
                        c.bankT[:, kk * 128:(kk + 1) * 128], mb[:, kb * 128:(kb + 1) * 128], c.identb[:, :]),
                        reads=[mb, c.identb], writes=[c.bankT])
                P.op("act", lambda e, kb0=kb0, nkk=nkk, sub=sub: e.activation(
                    out=maskT[:, kb0:kb0 + nkk, sub * 128:(sub + 1) * 128],
                    in_=c.bankT[:, 0:nkk * 128].rearrange("p (a b) -> p a b", a=nkk), func=AF.Copy),
                    reads=[c.bankT], writes=[maskT])

    def attn_head(j, h):
        qs = slice(j * 512, (j + 1) * 512)
        po = c.bank[3 + (cnt["acc"] % 2)]
        cnt["acc"] += 1
        nkb = 4 * j + 4
        s1, s2 = [], []
        for kb in range(nkb):
            ps = c.bank[cnt["s"] % 3]
            cnt["s"] += 1
            p_ = pt[cnt["p"] % 4]
            cnt["p"] += 1

            def f1(ps=ps, p_=p_, kb=kb):
                diag = kb >= 4 * j
                masked = diag if isB else True
                if isB:
                    P.op("pe", lambda e: e.matmul(ps[:, :], K[h][0:66, kb * 128:(kb + 1) * 128], Q[h][0:66, qs],
                                                  start=True, stop=not masked), reads=[K[h], Q[h]], writes=[ps])
                else:
                    P.op("pe", lambda e: e.matmul(ps[:, :], Kp[h // 2][:, kb * 128:(kb + 1) * 128], Qz[j % 2][:, h, :],
                                                  start=True, stop=False), reads=[Kp[h // 2], Qz[j % 2]], writes=[ps])
                if masked:
                    if isB:
                        r = kb - 4 * j
                        P.op("pe", lambda e: e.matmul(ps[:, :], c.identb[:, :], c.cmask[:, r, :], start=False, stop=True),
                             reads=[c.identb, c.cmask], writes=[ps])
                    else:
                        P.op("pe", lambda e: e.matmul(ps[:, :], c.identb[:, :], maskT[:, kb, :], start=False, stop=True),
                             reads=[c.identb, maskT], writes=[ps])
                if isB:
                    P.op("act", lambda e: e.activation(out=p_[:, :], in_=ps[:, :], func=AF.Exp, scale=0.125,
                                                       bias=c.ncum[:, h, kb:kb + 1]), reads=[ps, c.ncum], writes=[p_])
                else:
                    P.op("act", lambda e: e.activation(out=p_[:, :], in_=ps[:, :], func=AF.Exp, scale=0.125),
                         reads=[ps], writes=[p_])

            def f2(p_=p_, kb=kb):
                P.op("pe", lambda e: e.matmul(po[0:65, :], V[:, kb, h, :], p_[:, :], start=(kb == 0), stop=(kb == nkb - 1)),
                     reads=[V, p_], writes=[po])
            s1.append(f1)
            s2.append(f2)
        skew_emit([s1, s2], 2)
        P.op("act", lambda e: e.activation(out=bcs[64:65, :], in_=po[64:65, :], func=AF.Ln), reads=[po], writes=[bcs])
        P.op("act", lambda e: e.activation(out=bcs[64:65, :], in_=bcs[64:65, :], func=AF.Exp, scale=-1.0),
             reads=[bcs], writes=[bcs])
        P.op("pe", lambda e: e.matmul(epi[0:64, :], c.ones32[64:65, 0:64], bcs[64:65, :], start=True, stop=True),
             reads=[c.ones32, bcs], writes=[epi])
        P.op("act", lambda e: e.activation(out=bcs[0:64, :], in_=epi[0:64, :], func=AF.Copy), reads=[epi], writes=[bcs])
        s_ = st[(j * 4 + h) % 2]
        P.op("dve", lambda e: e.tensor_tensor(out=s_[0:64, :], in0=po[0:64, :], in1=bcs[0:64, :], op=ALU.mult),
             reads=[po, bcs], writes=[s_])
        P.dma("sp", c.yT[chunk_y0 + h // 2][(h % 2) * 64:(h % 2) * 64 + 64, qs], s_[0:64, :], reads=[s_], writes=[c.yT])

    if isB:
        for j in range(8):
            for h in range(4):
                attn_head(j, h)
    else:
        load_q(0)
        load_mb(0)
        transposes(0)
        for j in range(8):
            if j + 1 < 8:
                load_q(j + 1)
                load_mb(j + 1)
            for h in range(4):
                attn_head(j, h)
            if j + 1 < 8:
                transposes(j + 1)


def attn_C(P, c, l):
    A = c.arena
    A.reset()
    Qz = [A.alloc(f"cqz{b}", [128, 4, 512], BF16) for b in range(2)]
    Kp = [A.alloc(f"ck{i}", [128, S], BF16) for i in range(2)]
    V = A.alloc("cv", [128, 32, 256], BF16)
    E = [A.alloc(f"cE{i}", [128, 512], F32) for i in range(2)]
    L = [A.alloc(f"cL{i}", [128, 512], BF16) for i in range(4)]
    R = [A.alloc(f"cR{i}", [128, 512], BF16) for i in range(2)]
    pt = [A.alloc(f"pt{i}", [128, 512], BF16) for i in range(4)]
    st = [A.alloc(f"st{i}", [128, 512], BF16) for i in range(2)]
    for b in range(2):
        P.op("pool", lambda e, b=b: e.memset(Qz[b][:, :, :], 0.0), writes=[Qz[b]])
    c.idx.setup(A)

    def load_q(j):
        qs = slice(j * 512, (j + 1) * 512)
        for h in range(4):
            rows = slice((h % 2) * 64, (h % 2) * 64 + 64)
            P.dma("sp", Qz[j % 2][rows, h, :], c.fmq[G_CQ + h // 2][rows, qs], reads=[c.fmq], writes=[Qz[j % 2]])

    load_q(0)
    for i in range(2):
        P.dma("sp", Kp[i][:, :], c.fmq[G_CK + i], reads=[c.fmq], writes=[Kp[i]])
    P.dma("sp", V[:, :, :], c.vtm[:, 768:1024].rearrange("(kb p) c -> p kb c", p=128), reads=[c.vtm], writes=[V])
    cnt = {"s": 0, "n": 0, "acc": 0}
    for j in range(8):
        qs = slice(j * 512, (j + 1) * 512)
        if j + 1 < 8:
            load_q(j + 1)
        for h in range(4):
            Kh = Kp[h // 2]
            Qb = Qz[j % 2]
            po = c.bank[3 + (cnt["acc"] % 2)]
            Rb = R[cnt["acc"] % 2]
            cnt["acc"] += 1
            P.op("pool", lambda e, Rb=Rb: e.memset(Rb[:, :], 0.0), writes=[Rb])
            nkb = 4 * j + 4
            s1, s2, s3 = [], [], []
            for idx, kb in enumerate(range(nkb - 1, -1, -1)):
                ps = c.bank[cnt["s"] % 3]
                cnt["s"] += 1
                n = cnt["n"]
                cnt["n"] += 1
                Eb, Lb, p_ = E[n % 2], L[n % 4], pt[n % 4]

                def f1(ps=ps, Eb=Eb, Lb=Lb, kb=kb, h=h, j=j, Kh=Kh, Qb=Qb):
                    diag = kb >= 4 * j
                    r = kb - 4 * j
                    P.op("pe", lambda e: e.matmul(ps[:, :], Kh[:, kb * 128:(kb + 1) * 128], Qb[:, h, :], start=True, stop=False),
                         reads=[Kh, Qb], writes=[ps])
                    P.op("act", lambda e: e.activation(out=Eb[:, :], in_=ps[:, :], func=AF.Exp, scale=0.125),
                         reads=[ps], writes=[Eb])
                    P.op("act", lambda e: e.activation(out=Lb[:, :], in_=Eb[:, :], func=AF.Ln, bias=c.epsb[:, 2:3]),
                         reads=[Eb, c.epsb], writes=[Lb])
                    if diag:
                        P.op("dve", lambda e: e.tensor_tensor(out=Lb[:, :], in0=Lb[:, :], in1=c.cmask[:, 8 + r, :], op=ALU.mult),
                             reads=[Lb, c.cmask], writes=[Lb])

                def f2(ps=ps, Lb=Lb, p_=p_, kb=kb, j=j, Rb=Rb):
                    diag = kb >= 4 * j
                    r = kb - 4 * j
                    P.op("pe", lambda e: e.matmul(ps[:, :], c.trim8[:, :], Lb[:, :], start=False, stop=False, skip_group_check=True),
                         reads=[c.trim8, Lb], writes=[ps])
                    P.op("pe", lambda e: e.matmul(ps[:, :], c.onesm8[:, :], Rb[:, :], start=False, stop=not diag,
                                                  skip_group_check=True), reads=[c.onesm8, Rb], writes=[ps])
                    if diag:
                        P.op("pe", lambda e: e.matmul(ps[:, :], c.identb[:, :], c.cmask[:, 4 + r, :], start=False, stop=True,
                                                      skip_group_check=True), reads=[c.identb, c.cmask], writes=[ps])
                    P.op("act", lambda e: e.activation(out=p_[:, :], in_=ps[:, :], func=AF.Exp, scale=0.125),
                         reads=[ps], writes=[p_])
                    P.op("pool", lambda e: e.tensor_tensor(out=Rb[:, :], in0=Rb[:, :], in1=Lb[:, :], op=ALU.add),
                         reads=[Rb, Lb], writes=[Rb])

                def f3(p_=p_, kb=kb, h=h, idx=idx, nkb=nkb, po=po):
                    P.op("pe", lambda e: e.matmul(po[0:64, :], V[:, kb, h * 64:(h + 1) * 64], p_[:, :],
                                                  start=(idx == 0), stop=(idx == nkb - 1)), reads=[V, p_], writes=[po])
                s1.append(f1)
                s2.append(f2)
                s3.append(f3)
            skew_emit([s1, s2, s3], 1)
            s_ = st[(j * 4 + h) % 2]
            P.op("act", lambda e, po=po, s_=s_: e.activation(out=s_[0:64, :], in_=po[0:64, :], func=AF.Copy),
                 reads=[po], writes=[s_])
            P.dma("sp", c.yT[6 + h // 2][(h % 2) * 64:(h % 2) * 64 + 64, qs], s_[0:64, :], reads=[s_], writes=[c.yT])
            c.idx.slot((4 * j + 4) * 1.8)
    c.idx.slot(0.0, flush=True)


def forget_prepass(P, c, l):
    A = c.arena
    A.reset()
    f = A.alloc("fT", [4, S], F32)
    t = A.alloc("ftmp", [4, S], F32)
    one = A.alloc("fone", [4, S], F32)
    hi = A.alloc("fhi", [4, S], BF16)
    lo = A.alloc("flo", [4, S], BF16)
    nb = A.alloc("fnb", [4, 1], F32)
    bc0, _ = VEC_COLS[f"b_fgt{l}"]
    P.dma("sp", f[:, :], c.fT_s[0:4, :], reads=[c.fT_s], writes=[f])
    P.op("dve", lambda e: e.tensor_scalar(out=nb[:, :], in0=c.vecs[0:4, bc0:bc0 + 1], scalar1=-1.0, scalar2=None,
                                          op0=ALU.mult), reads=[c.vecs], writes=[nb])
    P.op("act", lambda e: e.activation(out=t[:, :], in_=f[:, :], func=AF.Exp, scale=-1.0, bias=nb[:, 0:1]),
         reads=[f, nb], writes=[t])
    P.op("act", lambda e: e.activation(out=t[:, :], in_=t[:, :], func=AF.Ln, bias=c.epsb[0:4, 2:3]),
         reads=[t, c.epsb], writes=[t])
    P.op("dve", lambda e: e.tensor_scalar(out=t[:, :], in0=t[:, :], scalar1=-1.0, scalar2=None, op0=ALU.mult),
         reads=[t], writes=[t])
    P.op("dve", lambda e: e.memset(one[:, :], 1.0), writes=[one])
    P.op("dve", lambda e: e.tensor_tensor_scan(out=f[:, :], data0=one[:, :], data1=t[:, :], initial=0.0,
                                               op0=ALU.mult, op1=ALU.add), reads=[one, t], writes=[f])
    pT = c.bank[0]
    for kb in range(32):
        P.op("pe", lambda e, kb=kb: e.transpose(pT[:, kb * 4:(kb + 1) * 4], f[0:4, kb * 128:(kb + 1) * 128],
                                                c.ident[0:4, 0:4]), reads=[f, c.ident], writes=[pT])
    P.op("dve", lambda e: e.tensor_scalar(out=c.ncum[:, :, :], in0=pT[:, 0:128].rearrange("p (kb h) -> p h kb", h=4),
                                          scalar1=-1.0, scalar2=None, op0=ALU.mult), reads=[pT], writes=[c.ncum])
    P.op("dve", lambda e: e.tensor_scalar(out=hi[:, :], in0=f[:, :], scalar1=8.0, scalar2=None, op0=ALU.mult),
         reads=[f], writes=[hi])
    P.op("dve", lambda e: e.scalar_tensor_tensor(out=lo[:, :], in0=f[:, :], scalar=8.0, in1=hi[:, :],
                                                 op0=ALU.mult, op1=ALU.subtract), reads=[f, hi], writes=[lo])
    P.dma("sp", c.caug_s[0:4, :], hi[:, :], reads=[hi], writes=[c.caug_s])
    P.dma("sp", c.caug_s[4:8, :], lo[:, :], reads=[lo], writes=[c.caug_s])


def merge_phase(P, c, l, x_src, x_dst):
    A = c.arena
    A.reset()
    xt = A.alloc("xt", [128, 4, D], F32)
    hT = A.alloc("hT", [128, 8, TT], BF16)
    yT = A.alloc("yTt", [128, 10, TT], BF16)
    mg = A.alloc("mg", [128, 4, D], F32)
    mT = A.alloc("mT", [128, 8, TT], BF16)
    sig = [A.alloc(f"sig{i}", [128, 512], F32) for i in range(2)]
    tmp = [A.alloc(f"mtmp{i}", [128, 512], F32) for i in range(2)]
    wg = A.alloc("wg", [128, 8, 4096], BF16)
    wb = A.alloc("wb", [128, 10, D], BF16)
    wo = A.alloc("wo", [128, 8, D], BF16)
    bg = A.alloc("bg", [1, 4096], BF16, parts=1)
    for i in range(4):
        P.dma("sp", wg[:, 2 * i:2 * i + 2, :], c.wbf[f"wgate{l}"][:, 2 * i:2 * i + 2, :], reads=[c.wbf[f"wgate{l}"]], writes=[wg])
    P.dma("sp", wb[:, :, :], c.wbf[f"wbr{l}"][:, :, :], reads=[c.wbf[f"wbr{l}"]], writes=[wb])
    P.dma("sp", wo[:, :, :], c.wbf[f"wout{l}"][:, :, :], reads=[c.wbf[f"wout{l}"]], writes=[wo])
    P.dma("sp", bg[:, :], c.wbf[f"bgate{l}"][:, :], reads=[c.wbf[f"bgate{l}"]], writes=[bg])
    chunks = [(0, 4), (4, 2), (6, 2), (8, 2)]
    n = 0
    for tt in range(NTT):
        tok = slice(tt * TT, (tt + 1) * TT)
        P.dma("sp", xt[:, :, :], x_tile_ap(x_src, tt), reads=[x_src], writes=[xt])
        P.dma("sp", hT[:, :, :], c.hT_s[tt], reads=[c.hT_s], writes=[hT])
        P.dma("sp", yT[:, :, :], c.yT[:, :, tok].rearrange("c p t -> p c t"), reads=[c.yT], writes=[yT])
        for i in range(4):
            ch0, nch = chunks[i]
            for dh in range(2):
                gc = slice(i * 1024 + dh * 512, i * 1024 + dh * 512 + 512)
                for tb in range(4):
                    ts_ = slice(tb * 128, (tb + 1) * 128)
                    pg = c.bank[(n % 2) * 2]
                    pb = c.bank[(n % 2) * 2 + 1]
                    sg_, tm_ = sig[n % 2], tmp[n % 2]
                    n += 1
                    for kc in range(8):
                        P.op("pe", lambda e, pg=pg, kc=kc, ts_=ts_, gc=gc: e.matmul(
                            pg[:, :], hT[:, kc, ts_], wg[:, kc, gc], start=(kc == 0), stop=False),
                            reads=[hT, wg], writes=[pg])
                    P.op("pe", lambda e, pg=pg, gc=gc: e.matmul(pg[:, :], c.onesb[0:1, :], bg[0:1, gc], start=False, stop=True),
                         reads=[c.onesb, bg], writes=[pg])
                    P.op("act", lambda e, pg=pg, sg_=sg_: e.activation(out=sg_[:, :], in_=pg[:, :], func=AF.Sigmoid),
                         reads=[pg], writes=[sg_])
                    for k in range(nch):
                        P.op("pe", lambda e, pb=pb, k=k, ch0=ch0, nch=nch, ts_=ts_, dh=dh: e.matmul(
                            pb[:, :], yT[:, ch0 + k, ts_], wb[:, ch0 + k, dh * 512:(dh + 1) * 512],
                            start=(k == 0), stop=(k == nch - 1)), reads=[yT, wb], writes=[pb])
                    ms_ = mg[:, tb, dh * 512:(dh + 1) * 512]
                    if i == 0:
                        P.op("dve", lambda e, ms_=ms_, sg_=sg_, pb=pb: e.tensor_tensor(out=ms_, in0=sg_[:, :], in1=pb[:, :],
                                                                                       op=ALU.mult), reads=[sg_, pb], writes=[mg])
                    else:
                        P.op("dve", lambda e, tm_=tm_, sg_=sg_, pb=pb: e.tensor_tensor(out=tm_[:, :], in0=sg_[:, :], in1=pb[:, :],
                                                                                       op=ALU.mult), reads=[sg_, pb], writes=[tm_])
                        P.op("pool", lambda e, ms_=ms_, tm_=tm_: e.tensor_tensor(out=ms_, in0=ms_, in1=tm_[:, :], op=ALU.add),
                             reads=[mg, tm_], writes=[mg])
        transpose_to_fm(P, c, mg, mT, None)
        for tb in range(4):
            for dh in range(2):
                po = c.bank[4 + (n % 2)]
                n += 1
                for kc in range(8):
                    P.op("pe", lambda e, po=po, kc=kc, tb=tb, dh=dh: e.matmul(
                        po[:, :], mT[:, kc, tb * 128:(tb + 1) * 128], wo[:, kc, dh * 512:(dh + 1) * 512],
                        start=(kc == 0), stop=(kc == 7)), reads=[mT, wo], writes=[po])
                P.op("dve", lambda e, po=po, tb=tb, dh=dh: e.tensor_tensor(
                    out=xt[:, tb, dh * 512:(dh + 1) * 512], in0=po[:, :], in1=xt[:, tb, dh * 512:(dh + 1) * 512],
                    op=ALU.add), reads=[po, xt], writes=[xt])
        P.dma("sp", x_tile_ap(x_dst, tt), xt[:, :, :], reads=[xt], writes=[x_dst])


ARENA_BYTES = 188 * 1024


def build_program(phases=("all",), debug=()):
    nc = bass.Bass("TRN2", target_bir_lowering=False)
    P = Prog(nc)
    c = Ctx()
    kd = lambda n: ("ExternalOutput" if n in debug else "Internal")
    c.x_in = P.dram("x", [S, D], F32, kind="ExternalInput")
    c.out = P.dram("out", [S, D], F32, kind="ExternalOutput")
    c.pos_in = P.dram("positions", [S], I32, kind="ExternalInput")
    nv = sum(v[1] for v in VEC_COLS.values())
    c.vecs_in = P.dram("vecs", [128, nv], F32, kind="ExternalInput")
    c.ident_in = P.dram("ident", [128, 128], F32, kind="ExternalInput")
    c.cmask_in = P.dram("cmask", [12, 128, 512], F32, kind="ExternalInput")
    c.cmat_in = P.dram("cmat", [4, 128, 128], F32, kind="ExternalInput")
    c.cqk_in = P.dram("cqk", [128, 128], F32, kind="ExternalInput")
    c.gfin_in = P.dram("final_norm", [D], F32, kind="ExternalInput")
    c.lam_in = {nm: P.dram(nm, [DEPTH, 64], F32, kind="ExternalInput") for nm in ("lam_q1", "lam_k1", "lam_q2", "lam_k2")}
    c.xres = P.dram("xres", [S, D], F32, kind=kd("xres"))
    c.hT_s = P.dram("hT_s", [NTT, 128, 8, TT], BF16, kind=kd("hT_s"))
    c.fmq = P.dram("fmq", [NG, 128, S], BF16, kind=kd("fmq"))
    c.fT_s = P.dram("fT_s", [4, S], F32, kind=kd("fT_s"))
    c.vtm = P.dram("vtm", [S, 1280], BF16, kind=kd("vtm"))
    c.iw_s = P.dram("iw_s", [S, 8], F32, kind=kd("iw_s"))
    c.caug_s = P.dram("caug_s", [8, S], BF16, kind=kd("caug_s"))
    c.yT = P.dram("yT", [10, 128, S], BF16, kind=kd("yT"))
    c.maskq = P.dram("maskq", [32, 128, S], BF16, kind=kd("maskq"))
    c.ropeC_s = P.dram("ropeC_s", [128, S], F32, kind=kd("ropeC_s"))
    c.ropeS_s = P.dram("ropeS_s", [128, S], F32, kind=kd("ropeS_s"))
    c.w32, c.wbf = {}, {}
    for name, shape in weight_specs():
        c.w32[name] = P.dram(name, shape, F32, kind="ExternalInput")
        c.wbf[name] = P.dram(name + "_bf", shape, BF16, kind="Internal")

    c.vecs = P.sb("vecs_sb", [128, nv], F32)
    c.ident = P.sb("ident_sb", [128, 128], F32)
    c.ones32 = P.sb("ones32", [128, 128], F32)
    c.cqk = P.sb("cqk_sb", [128, 128], F32)
    c.epsb = P.sb("epsb", [128, 4], F32)
    c.ms = P.sb("ms", [128, 12], F32)
    c.ncum = P.sb("ncum", [128, 4, 32], F32)
    c.cmask = P.sb("cmask_sb", [128, 12, 512], BF16)
    c.cmatb = P.sb("cmat_sb", [128, 4, 128], BF16)
    arena_t = nc.alloc_sbuf_tensor("arena", [128, ARENA_BYTES // 2], BF16)
    c.arena = Arena(P, arena_t, ARENA_BYTES)
    c.bank = [P.ps(f"bank{i}", [128, 512]) for i in range(7)]
    c.bankT = P.ps("bankT", [128, 1024], BF16)
    c.bank7 = Buf(c.bankT.t[:, :].bitcast(F32), "bank7")
    c.bank7.res = c.bankT.res

    class _V:
        pass
    mk = lambda i: type("B", (), {"t": c.cmatb.t, "res": c.cmatb.res, "__getitem__": lambda s, idx, i=i: c.cmatb.t[:, i, :][idx]})()
    c.identb, c.trim8, c.onesm8, c.onesb = mk(0), mk(1), mk(2), mk(3)

    P.dma("sp", c.vecs[:, :], c.vecs_in[:, :], reads=[c.vecs_in], writes=[c.vecs])
    P.dma("sp", c.ident[:, :], c.ident_in[:, :], reads=[c.ident_in], writes=[c.ident])
    P.dma("sp", c.cqk[:, :], c.cqk_in[:, :], reads=[c.cqk_in], writes=[c.cqk])
    P.dma("pool", c.cmask[:, :, :], c.cmask_in.t.rearrange("r p q -> p r q"), reads=[c.cmask_in], writes=[c.cmask])
    P.dma("pool", c.cmatb[:, :, :], c.cmat_in.t.rearrange("r p q -> p r q"), reads=[c.cmat_in], writes=[c.cmatb])
    P.op("dve", lambda e: e.memset(c.ones32[:, :], 1.0), writes=[c.ones32])
    P.op("dve", lambda e: e.memset(c.epsb[:, 0:1], 1e-6), writes=[c.epsb])
    P.op("dve", lambda e: e.memset(c.epsb[:, 1:2], 1e-5), writes=[c.epsb])
    P.op("dve", lambda e: e.memset(c.epsb[:, 2:3], 1.0), writes=[c.epsb])

    for name, shape in weight_specs():
        n = int(np.prod(shape))
        assert n % 2048 == 0, name
        rows = n // 2048
        src, dst = c.w32[name], c.wbf[name]
        pat = " ".join(f"a{i}" for i in range(len(shape)))
        s2 = src.t.rearrange(f"{pat} -> ({pat})").rearrange("(r c) -> r c", c=2048)
        d2 = dst.t.rearrange(f"{pat} -> ({pat})").rearrange("(r c) -> r c", c=2048)
        r0 = 0
        while r0 < rows:
            r1 = min(rows, r0 + 128)
            P.dma("pool", d2[r0:r1, :], s2[r0:r1, :], reads=[src], writes=[dst])
            r0 = r1

    ALL = "all" in phases
    on = lambda nm: ALL or nm in phases
    if on("rope"):
        rope_tables(P, c)
    cur = c.x_in
    for l in range(DEPTH):
        if on(f"ffn1_{l}"):
            ffn_phase(P, c, l, "ffn1", cur, c.xres)
            cur = c.xres
        if on(f"m1_{l}"):
            mix_proj_phase(P, c, l, cur)
        if on(f"fgt_{l}"):
            forget_prepass(P, c, l)
        c.idx = Indexer(P, c)
        if on(f"A_{l}"):
            attn_A(P, c, l)
        if on(f"B_{l}"):
            attn_BD(P, c, l, "B")
        if on(f"C_{l}"):
            attn_C(P, c, l)
        if on(f"D_{l}"):
            attn_BD(P, c, l, "D")
        if on(f"m3_{l}"):
            merge_phase(P, c, l, cur, c.xres)
            cur = c.xres
        if on(f"ffn2_{l}"):
            ffn_phase(P, c, l, "ffn2", cur, c.xres)
            cur = c.xres
    if on("final"):
        final_phase(P, c, cur, c.out)
    else:
        pass
    P.out_events += [(r.sem, r.semv) for r in P.dma_res]
    P.finish()
    return nc, P


def make_in_maps(inputs):
    x = np.asarray(inputs["x"], dtype=np.float32)
    vecs = build_vecs(inputs)
    hw = host_weights(inputs)
    cst = host_consts()
    maps = []
    for b in range(8):
        m = {"x": np.ascontiguousarray(x[b]),
             "positions": np.asarray(inputs["positions"], dtype=np.int32),
             "vecs": vecs, "final_norm": np.asarray(inputs["final_norm"], dtype=np.float32)}
        for nm in ("lam_q1", "lam_k1", "lam_q2", "lam_k2"):
            m[nm] = np.asarray(inputs[nm], dtype=np.float32)
        m.update(cst)
        m.update(hw)
        maps.append(m)
    return maps


def kernel(**inputs):
    maps = make_in_maps(inputs)
    nc, P = build_program()
    res = run_bass_kernel_spmd(nc, maps, core_ids=list(range(8)))
    return np.stack([res.results[b]["out"] for b in range(8)], axis=0).astype(np.float32)
```

```python
import numpy as np
import concourse.bass as bass
import concourse.mybir as mybir
from concourse.bass_utils import run_bass_kernel_spmd

F32 = mybir.dt.float32
BF16 = mybir.dt.bfloat16
I32 = mybir.dt.int32
AF = mybir.ActivationFunctionType
ALU = mybir.AluOpType

S = 4096
D = 1024
FF = 2816
NFC = FF // 128
DEPTH = 2
TT = 512
NTT = S // TT
NEG = -30000.0
IN_WIDTH = 4428


class Res:
    def __init__(self, name, multi=False):
        self.name = name
        self.w = []
        self.r = []
        self.sem = None
        self.semv = 0
        self.multi = multi


class Buf:
    def __init__(self, t, name, multi=False):
        self.t = t
        self.res = Res(name, multi)

    def __getitem__(self, idx):
        return self.t[idx]


class Prog:
    ENG = ("pe", "act", "dve", "pool", "sp")

    def __init__(self, nc):
        self.nc = nc
        self.q = {e: [] for e in self.ENG}
        self.sem = {e: nc.alloc_semaphore("sem_" + e) for e in ("pe", "act", "dve", "pool")}
        self.cnt = {e: 0 for e in self.sem}
        self.seen = {e: {} for e in self.ENG}
        self.nsem = 4
        self.out_events = []
        self.ninst = 0
        self.dma_res = []
        self.sem_pool = []

    def sb(self, name, shape, dt, multi=False):
        return Buf(self.nc.alloc_sbuf_tensor(name, list(shape), dt), name, multi)

    def ps(self, name, shape, dt=F32):
        return Buf(self.nc.alloc_psum_tensor(name, list(shape), dt), name)

    def dram(self, name, shape, dt, kind="Internal", multi=True):
        t = self.nc.dram_tensor(name, list(shape), dt, kind=kind)
        b = Buf(t.ap(), name, multi)
        return b

    def _deps(self, reads, writes, is_dma=False, eng=None):
        ev = []
        own = self.sem.get(eng)
        for b in reads:
            ev.extend(b.res.w)
        for b in writes:
            r = b.res
            if not (r.multi and is_dma):
                ev.extend(x for x in r.w if x[0] is not own)
            ev.extend(r.r)
        return ev

    def _wait(self, eng, events):
        seen = self.seen[eng]
        best = {}
        for (sem, val) in events:
            k = sem.num
            if seen.get(k, 0) >= val:
                continue
            if k not in best or best[k][1] < val:
                best[k] = (sem, val)
        for k, (sem, val) in best.items():
            self.q[eng].append(("w", sem, val))
            seen[k] = val

    def _record(self, ev, reads, writes, is_dma=False):
        for b in writes:
            r = b.res
            r.w = [ev]
            r.r = []
        for b in reads:
            b.res.r.append(ev)

    def op(self, eng, fn, reads=(), writes=()):
        self._wait(eng, self._deps(reads, writes, eng=eng))
        self.cnt[eng] += 1
        ev = (self.sem[eng], self.cnt[eng])
        self.q[eng].append(("i", fn, self.sem[eng], 1))
        self._record(ev, reads, writes)
        self.ninst += 1
        return ev

    def dma(self, eng, out_ap, in_ap, reads=(), writes=(), is_out=False, **kw):
        self._wait(eng, self._deps(reads, writes, is_dma=True))
        r = writes[0].res
        if r.sem is None:
            if self.sem_pool:
                r.sem, r.semv = self.sem_pool.pop()
            else:
                r.sem = self.nc.alloc_semaphore(f"dsem{self.nsem}")
                self.nsem += 1
            self.dma_res.append(r)
        r.semv += 16
        ev = (r.sem, r.semv)
        self.q[eng].append(("i", lambda e: e.dma_start(out=out_ap, in_=in_ap, **kw), r.sem, 16))
        self._record(ev, reads, writes, is_dma=True)
        if is_out:
            self.out_events.append(ev)
        self.ninst += 1
        return ev

    def barrier(self):
        evs = [(self.sem[e], self.cnt[e]) for e in self.sem if self.cnt[e] > 0]
        evs += [(r.sem, r.semv) for r in self.dma_res if not getattr(r, "no_barrier", False)]
        for eng in self.ENG:
            self._wait(eng, evs)

    def release(self, bufs):
        for b in bufs:
            r = b.res
            if r.sem is not None:
                self.sem_pool.append((r.sem, r.semv))
                self.dma_res.remove(r)
                r.sem = None

    def finish(self):
        nc = self.nc
        self._wait("sp", self.out_events)
        q = self.q

        def replay(lst, e):
            for it in lst:
                if it[0] == "w":
                    e.wait_ge(it[1], it[2])
                else:
                    it[1](e).then_inc(it[2], it[3])

        with nc.Block() as block:
            @block.tensor
            def _(e):
                replay(q["pe"], e)

            @block.scalar
            def _(e):
                replay(q["act"], e)

            @block.vector
            def _(e):
                replay(q["dve"], e)

            @block.gpsimd
            def _(e):
                replay(q["pool"], e)

            @block.sync
            def _(e):
                replay(q["sp"], e)


def _kc_tile(w):
    k, n = w.shape
    return np.ascontiguousarray(w.reshape(k // 128, 128, n).transpose(1, 0, 2))


def lay_wgu(w):
    t = _kc_tile(w)
    g = t[:, :, :FF].reshape(128, 8, NFC, 128)
    u = t[:, :, FF:].reshape(128, 8, NFC, 128)
    gu = np.concatenate([g, u], axis=3)
    return np.ascontiguousarray(gu.transpose(2, 0, 1, 3))


def lay_wd(w):
    return _kc_tile(w)


FM_GROUPS = ([(0 + 128 * i, 128, True) for i in range(4)] +
             [(512 + 128 * i, 128, True) for i in range(4)] +
             [(1536 + 128 * i, 128, False) for i in range(2)] +
             [(1792 + 128 * i, 128, False) for i in range(2)] +
             [(2308 + 128 * i, 128, False) for i in range(2)] +
             [(2564 + 128 * i, 128, False) for i in range(2)] +
             [(3076 + 128 * i, 128, True) for i in range(2)] +
             [(3332 + 128 * i, 128, True) for i in range(2)] +
             [(3844 + 128 * i, 128, True) for i in range(4)] +
             [(4356, 64, True)] +
             [(2304, 4, False)])
NG = len(FM_GROUPS)
G_AQ, G_AK, G_BQ, G_BK, G_CQ, G_CK, G_DQ, G_DK, G_IQ, G_IK, G_BF = 0, 4, 8, 10, 12, 14, 16, 18, 20, 24, 25
TM_COLS = [(1024, 512), (2048, 256), (2820, 256), (3588, 256), (4420, 8)]
NTM = 1288


def lay_wfm(w):
    t = _kc_tile(w)
    out = np.zeros((NG, 128, 8, 256), np.float32)
    for g, (c0, n, rot) in enumerate(FM_GROUPS):
        out[g, :, :, 0:n] = t[:, :, c0:c0 + n]
        if rot:
            for m in range(n // 64):
                b = c0 + m * 64
                o = 128 + m * 64
                out[g, :, :, o:o + 8] = t[:, :, b + 8:b + 16]
                out[g, :, :, o + 8:o + 16] = t[:, :, b:b + 8]
    return out


def lay_wtm(w):
    t = _kc_tile(w)
    return np.ascontiguousarray(np.concatenate([t[:, :, c0:c0 + n] for c0, n in TM_COLS], axis=2))


VEC_COLS = {}
NIT = 24


def build_vecs(inp):
    cols = []

    def add(name, arr):
        VEC_COLS[name] = (sum(c.shape[1] for c in cols), arr.shape[1])
        cols.append(np.ascontiguousarray(arr, dtype=np.float32))

    for l in range(DEPTH):
        for nm in ("ffn1_norm", "mix_norm", "ffn2_norm"):
            add(f"{nm}{l}", np.asarray(inp[nm])[l].reshape(8, 128).T)
        add(f"diff_gain{l}", np.asarray(inp["diff_gain"])[l].reshape(128, 1))
        bf = np.zeros((128, 1), np.float32)
        bf[0:4, 0] = np.asarray(inp["b_fgt"])[l]
        add(f"b_fgt{l}", bf)
    add("final_norm", np.asarray(inp["final_norm"]).reshape(8, 128).T)
    fr = np.zeros((128, 2), np.float32)
    freqs = (500000.0 ** (-np.arange(0, 16, 2, dtype=np.float32) / np.float32(16))).astype(np.float32)
    for m in range(2):
        for i in range(8):
            fr[m * 64 + i, 0] = freqs[i]
            fr[m * 64 + 8 + i, 0] = freqs[i]
            fr[m * 64 + i, 1] = -1.0
            fr[m * 64 + 8 + i, 1] = 1.0
    add("rope", fr)
    add("pw", np.tile((2.0 ** -(np.arange(NIT) + 1.0)).astype(np.float32)[None, :], (128, 1)))
    add("cntb", np.tile(((np.arange(32) + 1.0) * 128.0 - 512.0 + 0.5).astype(np.float32)[None, :], (128, 1)))
    return np.concatenate(cols, axis=1)


def host_consts():
    k = np.arange(128)[:, None]
    q = np.arange(512)[None, :]
    cmask = np.zeros((12, 128, 512), np.float32)
    for r in range(4):
        cmask[r] = np.where(128 * r + k <= q, 0.0, NEG)
        cmask[4 + r] = np.where(128 * r + k < q, 0.0, NEG)
        cmask[8 + r] = np.where(128 * r + k < q, 1.0, 0.0)
    j = np.arange(128)[:, None]
    s = np.arange(128)[None, :]
    cmat = np.zeros((4, 128, 128), np.float32)
    cmat[0] = np.eye(128)
    cmat[1] = np.where(j >= s, -8.0, 0.0)
    cmat[2] = -8.0
    cmat[3] = 1.0
    cqk = np.where(s <= j, 0.0, -1e30).astype(np.float32)
    return {"ident": np.eye(128, dtype=np.float32), "cmask": cmask, "cmat": cmat, "cqk": cqk}


def weight_specs():
    specs = []
    for l in range(DEPTH):
        specs.append((f"ffn1_wgu{l}", (NFC, 128, 8, 256)))
        specs.append((f"ffn1_wd{l}", (128, NFC, D)))
        specs.append((f"wfm{l}", (NG, 128, 8, 256)))
        specs.append((f"wtm{l}", (128, 8, NTM)))
        specs.append((f"wgate{l}", (128, 8, 4096)))
        specs.append((f"wbr{l}", (128, 10, D)))
        specs.append((f"wout{l}", (128, 8, D)))
        specs.append((f"bgate{l}", (1, 4096)))
        specs.append((f"ffn2_wgu{l}", (NFC, 128, 8, 256)))
        specs.append((f"ffn2_wd{l}", (128, NFC, D)))
    return specs


def host_weights(inp):
    out = {}
    A = lambda k: np.asarray(inp[k], dtype=np.float32)
    for l in range(DEPTH):
        for which in ("ffn1", "ffn2"):
            out[f"{which}_wgu{l}"] = lay_wgu(A(f"{which}_w_gu")[l])
            out[f"{which}_wd{l}"] = lay_wd(A(f"{which}_w_down")[l])
        out[f"wfm{l}"] = lay_wfm(A("w_in")[l])
        out[f"wtm{l}"] = lay_wtm(A("w_in")[l])
        out[f"wgate{l}"] = _kc_tile(A("w_gate")[l])
        out[f"wbr{l}"] = _kc_tile(np.concatenate([A("w_br_a")[l], A("w_br_b")[l], A("w_br_c")[l], A("w_br_d")[l]], axis=0))
        out[f"wout{l}"] = _kc_tile(A("w_out")[l])
        out[f"bgate{l}"] = np.ascontiguousarray(A("b_gate")[l].reshape(1, 4096))
    return out


class Ctx:
    pass


class Arena:
    def __init__(self, P, t, nbytes):
        self.P, self.t, self.nbytes, self.off = P, t, nbytes, 0
        self.bufs = []

    def reset(self):
        self.P.barrier()
        self.P.release(self.bufs)
        self.bufs = []
        self.off = 0

    def alloc(self, name, shape, dt, parts=128):
        esz = 4 if dt in (F32, I32) else 2
        n = int(np.prod(shape[1:]))
        nb = (n * esz + 63) // 64 * 64
        assert self.off + nb <= self.nbytes, (name, self.off, nb, self.nbytes)
        ap = self.t[0:shape[0], self.off // 2:(self.off + nb) // 2]
        if esz == 4:
            ap = ap.bitcast(dt)
        ap = ap[:, 0:n]
        if len(shape) == 3:
            ap = ap.rearrange("p (a b) -> p a b", a=shape[1])
        elif len(shape) == 4:
            ap = ap.rearrange("p (a b c) -> p a b c", a=shape[1], b=shape[2])
        self.off += nb
        b = Buf(ap, name)
        self.bufs.append(b)
        return b


def x_tile_ap(x, tt):
    return x[tt * TT:(tt + 1) * TT, :].rearrange("(tb p) d -> p tb d", p=128)


def rms_stats(P, c, xb, eps_col):
    for tb in range(4):
        P.op("act", lambda e, tb=tb: e.activation(
            out=c.xn[:, tb, :], in_=xb[:, tb, :], func=AF.Square, accum_out=c.ms[:, tb:tb + 1]),
            reads=[xb], writes=[c.xn, c.ms])
    P.op("act", lambda e: e.activation(out=c.ms[:, 4:8], in_=c.ms[:, 0:4], func=AF.Ln, bias=eps_col,
                                       scale=1.0 / D),
         reads=[c.ms, c.epsb], writes=[c.ms])
    P.op("act", lambda e: e.activation(out=c.ms[:, 8:12], in_=c.ms[:, 4:8], func=AF.Exp, scale=-0.5),
         reads=[c.ms], writes=[c.ms])


def transpose_to_fm(P, c, src, dstT, scale_col0=None):
    for kc in range(8):
        pT = c.bank[kc % 2]
        for tb in range(4):
            P.op("pe", lambda e, pT=pT, tb=tb, kc=kc: e.transpose(
                pT[:, tb * 128:(tb + 1) * 128], src[:, tb, kc * 128:(kc + 1) * 128], c.ident[:, :]),
                reads=[src, c.ident], writes=[pT])
        if scale_col0 is None:
            P.op("act", lambda e, pT=pT, kc=kc: e.activation(out=dstT[:, kc, :], in_=pT[:, :], func=AF.Copy),
                 reads=[pT], writes=[dstT])
        else:
            P.op("act", lambda e, pT=pT, kc=kc: e.activation(
                out=dstT[:, kc, :], in_=pT[:, :], func=AF.Copy,
                scale=c.vecs[:, scale_col0 + kc:scale_col0 + kc + 1]),
                reads=[pT, c.vecs], writes=[dstT])


def norm_transpose(P, c, xb, gcol0):
    rms_stats(P, c, xb, c.epsb[:, 0:1])
    for tb in range(4):
        P.op("dve", lambda e, tb=tb: e.tensor_scalar(
            out=c.xn[:, tb, :], in0=xb[:, tb, :], scalar1=c.ms[:, 8 + tb:9 + tb], scalar2=None,
            op0=ALU.mult), reads=[xb, c.ms], writes=[c.xn])
    transpose_to_fm(P, c, c.xn, c.hT, gcol0)


def alloc_stream_common(c, A):
    c.xt = [A.alloc(f"xt{i}", [128, 4, D], F32) for i in range(2)]
    c.xn = A.alloc("xn", [128, 4, D], F32)
    c.hT = A.alloc("hT", [128, 8, TT], BF16)
    c.wring = [A.alloc(f"wr{i}", [128, 2048], BF16) for i in range(3)]


def ffn_phase(P, c, l, which, x_src, x_dst):
    A = c.arena
    A.reset()
    alloc_stream_common(c, A)
    c.aT = A.alloc("aT", [128, NFC, TT], BF16)
    c.wd = A.alloc("wd", [128, NFC, D], BF16)
    c.sg = [A.alloc(f"sg{i}", [128, TT], F32) for i in range(2)]
    wgu = c.wbf[f"{which}_wgu{l}"]
    wdn = c.wbf[f"{which}_wd{l}"]
    gcol0, _ = VEC_COLS[f"{which}_norm{l}"]

    for j in range(2):
        P.dma("sp", c.wd[:, j * 11:(j + 1) * 11, :], wdn[:, j * 11:(j + 1) * 11, :],
              reads=[wdn], writes=[c.wd])

    def load_x(tt):
        xb = c.xt[tt % 2]
        P.dma("sp", xb[:, :, :], x_tile_ap(x_src, tt), reads=[x_src], writes=[xb])

    nw = NTT * NFC
    wstate = {"next": 0}

    def issue_w(upto):
        while wstate["next"] <= upto and wstate["next"] < nw:
            i = wstate["next"]
            fc = i % NFC
            slot = c.wring[i % 3]
            P.dma("sp", slot[:, :], wgu[fc].rearrange("p k c -> p (k c)"), reads=[wgu], writes=[slot])
            wstate["next"] += 1

    load_x(0)
    for tt in range(NTT):
        xb = c.xt[tt % 2]
        if tt + 1 < NTT:
            load_x(tt + 1)
        issue_w(tt * NFC + 1)
        norm_transpose(P, c, xb, gcol0)
        for fc in range(NFC):
            i = tt * NFC + fc
            issue_w(i + 2)
            slot = c.wring[i % 3]
            w3 = slot[:, :].rearrange("p (k c) -> p k c", k=8)
            pg = c.bank[2 + 2 * (fc % 2)]
            pu = c.bank[3 + 2 * (fc % 2)]
            for half, pb in ((0, pg), (1, pu)):
                for kc in range(8):
                    P.op("pe", lambda e, pb=pb, w3=w3, kc=kc, half=half: e.matmul(
                        pb[:, :], w3[:, kc, half * 128:(half + 1) * 128], c.hT[:, kc, :],
                        start=(kc == 0), stop=(kc == 7)),
                        reads=[slot, c.hT], writes=[pb])
            sg = c.sg[fc % 2]
            P.op("act", lambda e, sg=sg, pg=pg: e.activation(out=sg[:, :], in_=pg[:, :], func=AF.Silu),
                 reads=[pg], writes=[sg])
            P.op("dve", lambda e, sg=sg, pu=pu, fc=fc: e.tensor_tensor(
                out=c.aT[:, fc, :], in0=sg[:, :], in1=pu[:, :], op=ALU.mult),
                reads=[sg, pu], writes=[c.aT])
        n = 0
        for tb in range(4):
            for dh in range(2):
                po = c.bank[n % 2]
                n += 1
                for fc in range(NFC):
                    P.op("pe", lambda e, po=po, fc=fc, tb=tb, dh=dh: e.matmul(
                        po[:, :], c.aT[:, fc, tb * 128:(tb + 1) * 128],
                        c.wd[:, fc, dh * 512:(dh + 1) * 512],
                        start=(fc == 0), stop=(fc == NFC - 1)),
                        reads=[c.aT, c.wd], writes=[po])
                P.op("dve", lambda e, po=po, tb=tb, dh=dh, xb=xb: e.scalar_tensor_tensor(
                    out=xb[:, tb, dh * 512:(dh + 1) * 512], in0=po[:, :], scalar=0.5,
                    in1=xb[:, tb, dh * 512:(dh + 1) * 512], op0=ALU.mult, op1=ALU.add),
                    reads=[po, xb], writes=[xb])
        P.dma("sp", x_tile_ap(x_dst, tt), xb[:, :, :], reads=[xb], writes=[x_dst])


def final_phase(P, c, x_src, out):
    A = c.arena
    A.reset()
    alloc_stream_common(c, A)
    gfin = A.alloc("gfin", [128, D], F32)
    P.dma("sp", gfin[:, :], c.gfin_in[:].partition_broadcast(128), reads=[c.gfin_in], writes=[gfin])
    for tt in range(NTT):
        xb = c.xt[tt % 2]
        P.dma("sp", xb[:, :, :], x_tile_ap(x_src, tt), reads=[x_src], writes=[xb])
        rms_stats(P, c, xb, c.epsb[:, 0:1])
        for tb in range(4):
            P.op("dve", lambda e, tb=tb, xb=xb: e.scalar_tensor_tensor(
                out=xb[:, tb, :], in0=xb[:, tb, :], scalar=c.ms[:, 8 + tb:9 + tb], in1=gfin[:, :],
                op0=ALU.mult, op1=ALU.mult), reads=[xb, c.ms, gfin], writes=[xb])
        P.dma("sp", x_tile_ap(out, tt), xb[:, :, :], reads=[xb], writes=[out], is_out=True)


def rope_tables(P, c):
    A = c.arena
    A.reset()
    posi = A.alloc("posi", [128, S], I32)
    ang = A.alloc("ang", [128, S], F32)
    t1 = A.alloc("t1", [128, S], F32)
    ki = A.alloc("ki", [128, S], I32)
    t2 = A.alloc("t2", [128, S], F32)
    rc0, _ = VEC_COLS["rope"]
    PI = float(np.pi)
    TWO_PI = float(2 * np.pi)
    P.dma("sp", posi[:, :], c.pos_in[:].partition_broadcast(128), reads=[c.pos_in], writes=[posi])
    P.op("dve", lambda e: e.tensor_copy(out=ang[:, :], in_=posi[:, :]), reads=[posi], writes=[ang])
    P.op("dve", lambda e: e.tensor_scalar(out=ang[:, :], in0=ang[:, :], scalar1=c.vecs[:, rc0:rc0 + 1],
                                          scalar2=None, op0=ALU.mult), reads=[ang, c.vecs], writes=[ang])
    for which, shift, dst in (("s", 0.0, c.ropeS_s), ("c", PI / 2, c.ropeC_s)):
        P.op("dve", lambda e, shift=shift: e.tensor_scalar(out=t1[:, :], in0=ang[:, :], scalar1=shift, scalar2=None,
                                                           op0=ALU.add), reads=[ang], writes=[t1])
        P.op("dve", lambda e: e.tensor_scalar(out=ki[:, :], in0=t1[:, :], scalar1=1.0 / TWO_PI, scalar2=None,
                                              op0=ALU.mult), reads=[t1], writes=[ki])
        P.op("dve", lambda e: e.tensor_copy(out=t2[:, :], in_=ki[:, :]), reads=[ki], writes=[t2])
        P.op("dve", lambda e: e.scalar_tensor_tensor(out=t1[:, :], in0=t2[:, :], scalar=-TWO_PI, in1=t1[:, :],
                                                     op0=ALU.mult, op1=ALU.add), reads=[t2, t1], writes=[t1])
        P.op("dve", lambda e: e.tensor_scalar(out=t2[:, :], in0=t1[:, :], scalar1=PI, scalar2=-TWO_PI,
                                              op0=ALU.is_gt, op1=ALU.mult), reads=[t1], writes=[t2])
        P.op("dve", lambda e: e.tensor_tensor(out=t1[:, :], in0=t1[:, :], in1=t2[:, :], op=ALU.add),
             reads=[t1, t2], writes=[t1])
        P.op("dve", lambda e: e.tensor_scalar(out=t2[:, :], in0=t1[:, :], scalar1=-PI, scalar2=TWO_PI,
                                              op0=ALU.is_lt, op1=ALU.mult), reads=[t1], writes=[t2])
        P.op("dve", lambda e: e.tensor_tensor(out=t1[:, :], in0=t1[:, :], in1=t2[:, :], op=ALU.add),
             reads=[t1, t2], writes=[t1])
        P.op("dve", lambda e: e.tensor_scalar(out=t1[:, :], in0=t1[:, :], scalar1=PI, scalar2=-PI,
                                              op0=ALU.min, op1=ALU.max), reads=[t1], writes=[t1])
        P.op("act", lambda e: e.activation(out=t2[:, :], in_=t1[:, :], func=AF.Sin),
             reads=[t1], writes=[t2])
        if which == "s":
            P.op("dve", lambda e: e.tensor_scalar(out=t2[:, :], in0=t2[:, :], scalar1=c.vecs[:, rc0 + 1:rc0 + 2],
                                                  scalar2=None, op0=ALU.mult), reads=[t2, c.vecs], writes=[t2])
        P.dma("sp", dst[:, :], t2[:, :], reads=[t2], writes=[dst])


def mix_proj_phase(P, c, l, x_src):
    A = c.arena
    A.reset()
    alloc_stream_common(c, A)
    wtm_sb = A.alloc("wtm", [128, 8, NTM], BF16)
    fst = [A.alloc(f"fst{i}", [128, TT], BF16) for i in range(3)]
    f32st = A.alloc("f32st", [128, TT], F32)
    r1 = [A.alloc(f"r1_{i}", [128, TT], F32) for i in range(2)]
    r2 = [A.alloc(f"r2_{i}", [128, TT], F32) for i in range(2)]
    vst = A.alloc("vst", [128, 4, 1280], BF16)
    iwst = A.alloc("iwst", [128, 4, 8], F32)
    rcs = [A.alloc(f"rc{i}", [128, TT], F32) for i in range(2)]
    rss = [A.alloc(f"rs{i}", [128, TT], F32) for i in range(2)]
    wfm = c.wbf[f"wfm{l}"]
    wtm = c.wbf[f"wtm{l}"]
    gcol0, _ = VEC_COLS[f"mix_norm{l}"]
    P.dma("sp", wtm_sb[:, :, :], wtm[:, :, :], reads=[wtm], writes=[wtm_sb])

    def load_x(tt):
        xb = c.xt[tt % 2]
        P.dma("sp", xb[:, :, :], x_tile_ap(x_src, tt), reads=[x_src], writes=[xb])

    nw = NTT * NG
    wstate = {"next": 0}

    def issue_w(upto):
        while wstate["next"] <= upto and wstate["next"] < nw:
            i = wstate["next"]
            slot = c.wring[i % 3]
            P.dma("sp", slot[:, :], wfm[i % NG].rearrange("p k c -> p (k c)"), reads=[wfm], writes=[slot])
            wstate["next"] += 1

    load_x(0)
    nst = 0
    for tt in range(NTT):
        xb = c.xt[tt % 2]
        tok = slice(tt * TT, (tt + 1) * TT)
        if tt + 1 < NTT:
            load_x(tt + 1)
        issue_w(tt * NG + 1)
        rc_, rs_ = rcs[tt % 2], rss[tt % 2]
        P.dma("sp", rc_[:, :], c.ropeC_s[:, tok], reads=[c.ropeC_s], writes=[rc_])
        P.dma("sp", rs_[:, :], c.ropeS_s[:, tok], reads=[c.ropeS_s], writes=[rs_])
        norm_transpose(P, c, xb, gcol0)
        P.dma("sp", c.hT_s[tt], c.hT[:, :, :], reads=[c.hT], writes=[c.hT_s])
        for g in range(NG):
            i = tt * NG + g
            issue_w(i + 2)
            slot = c.wring[i % 3]
            w3 = slot[:, :].rearrange("p (k c) -> p k c", k=8)
            rot = FM_GROUPS[g][2]
            pm = c.bank[2 + 2 * (g % 2)]
            pr = c.bank[3 + 2 * (g % 2)]
            for half, pb in (((0, pm), (1, pr)) if rot else ((0, pm),)):
                for kc in range(8):
                    P.op("pe", lambda e, pb=pb, w3=w3, kc=kc, half=half: e.matmul(
                        pb[:, :], w3[:, kc, half * 128:(half + 1) * 128], c.hT[:, kc, :],
                        start=(kc == 0), stop=(kc == 7)),
                        reads=[slot, c.hT], writes=[pb])
            if g == G_BF:
                P.op("act", lambda e, pm=pm: e.activation(out=f32st[:, :], in_=pm[:, :], func=AF.Copy),
                     reads=[pm], writes=[f32st])
                P.dma("sp", c.fT_s[0:4, tok], f32st[0:4, :], reads=[f32st], writes=[c.fT_s])
                continue
            st = fst[nst % 3]
            nst += 1
            if rot:
                a1, a2 = r1[g % 2], r2[g % 2]
                P.op("dve", lambda e, pm=pm, a1=a1, rc_=rc_: e.tensor_tensor(
                    out=a1[:, :], in0=pm[:, :], in1=rc_[:, :], op=ALU.mult),
                    reads=[pm, rc_], writes=[a1])
                P.op("dve", lambda e, pr=pr, a2=a2, rs_=rs_: e.tensor_tensor(
                    out=a2[:, :], in0=pr[:, :], in1=rs_[:, :], op=ALU.mult),
                    reads=[pr, rs_], writes=[a2])
                P.op("pool", lambda e, a1=a1, a2=a2, st=st: e.tensor_tensor(
                    out=st[:, :], in0=a1[:, :], in1=a2[:, :], op=ALU.add),
                    reads=[a1, a2], writes=[st])
            else:
                P.op("act", lambda e, pm=pm, st=st: e.activation(out=st[:, :], in_=pm[:, :], func=AF.Copy),
                     reads=[pm], writes=[st])
            P.dma("sp", c.fmq[g][:, tok], st[:, :], reads=[st], writes=[c.fmq])
        nb = 0
        for tb in range(4):
            for (c0, n) in ((0, 512), (512, 512), (1024, 264)):
                pb = c.bank[nb % 2]
                nb += 1
                for kc in range(8):
                    P.op("pe", lambda e, pb=pb, kc=kc, tb=tb, c0=c0, n=n: e.matmul(
                        pb[:, 0:n], c.hT[:, kc, tb * 128:(tb + 1) * 128], wtm_sb[:, kc, c0:c0 + n],
                        start=(kc == 0), stop=(kc == 7)),
                        reads=[c.hT, wtm_sb], writes=[pb])
                nv = min(n, 1280 - c0)
                P.op("act", lambda e, pb=pb, tb=tb, c0=c0, nv=nv: e.activation(
                    out=vst[:, tb, c0:c0 + nv], in_=pb[:, 0:nv], func=AF.Copy),
                    reads=[pb], writes=[vst])
                if c0 == 1024:
                    P.op("act", lambda e, pb=pb, tb=tb: e.activation(
                        out=iwst[:, tb, :], in_=pb[:, 256:264], func=AF.Copy),
                        reads=[pb], writes=[iwst])
        P.dma("sp", c.vtm[tok, :].rearrange("(tb p) c -> p tb c", p=128), vst[:, :, :],
              reads=[vst], writes=[c.vtm])
        P.dma("sp", c.iw_s[tok, :].rearrange("(tb p) c -> p tb c", p=128), iwst[:, :, :],
              reads=[iwst], writes=[c.iw_s])


def skew_emit(stage_lists, lag):
    ns = len(stage_lists)
    n = len(stage_lists[0])
    for step in range(n + (ns - 1) * lag):
        for k in range(ns):
            t = step - k * lag
            if 0 <= t < n:
                stage_lists[k][t]()


def attn_A(P, c, l):
    A = c.arena
    A.reset()
    lam_init = 0.8 - 0.6 * float(np.exp(-0.3 * l))
    Qz = [A.alloc(f"aqz{b}", [128, 8, 512], BF16) for b in range(2)]
    K = [A.alloc(f"ak{h}", [128, S], BF16) for h in range(4)]
    V = A.alloc("av", [128, 32, 512], BF16)
    pt = [A.alloc(f"pt{i}", [128, 512], BF16) for i in range(4)]
    bcs = [A.alloc(f"bcs{i}", [128, 512], F32) for i in range(2)]
    tt_ = [A.alloc(f"tA{i}", [128, 512], F32) for i in range(2)]
    yb = A.alloc("yA", [128, 512], F32)
    sq = A.alloc("sqA", [128, 512], F32)
    rs = A.alloc("rsA", [128, 512], F32)
    st = [A.alloc(f"stA{i}", [128, 512], BF16) for i in range(2)]
    lamv = A.alloc("lamv", [128, 4, 64], F32)
    lamt = A.alloc("lamt", [128, 2, 64], F32)
    lams = A.alloc("lams", [128, 8], F32)
    for b in range(2):
        P.op("pool", lambda e, b=b: e.memset(Qz[b][:, :, :], 0.0), writes=[Qz[b]])

    def load_q(j):
        qs = slice(j * 512, (j + 1) * 512)
        for h in range(4):
            for m in range(2):
                rows = slice(m * 64, (m + 1) * 64)
                P.dma("sp", Qz[j % 2][rows, h * 2 + m, :], c.fmq[G_AQ + h][rows, qs], reads=[c.fmq], writes=[Qz[j % 2]])

    load_q(0)
    for h in range(4):
        P.dma("sp", K[h][:, :], c.fmq[G_AK + h], reads=[c.fmq], writes=[K[h]])
    P.dma("sp", V[:, :, :], c.vtm[:, 0:512].rearrange("(kb p) c -> p kb c", p=128), reads=[c.vtm], writes=[V])
    for i, nm in enumerate(("lam_q1", "lam_k1", "lam_q2", "lam_k2")):
        P.dma("sp", lamv[:, i, :], c.lam_in[nm][l, :].partition_broadcast(128), reads=[c.lam_in[nm]], writes=[lamv])
    for i in range(2):
        P.op("dve", lambda e, i=i: e.tensor_tensor(out=lamt[:, i, :], in0=lamv[:, 2 * i, :], in1=lamv[:, 2 * i + 1, :],
                                                   op=ALU.mult), reads=[lamv], writes=[lamt])
        P.op("dve", lambda e, i=i: e.tensor_scalar(out=lamv[:, i, :], in0=lamt[:, i, :], scalar1=1.0, scalar2=None,
                                                   op0=ALU.mult, op1=ALU.add, accum_out=lams[:, i:i + 1]),
             reads=[lamt], writes=[lamv, lams])
    P.op("act", lambda e: e.activation(out=lams[:, 2:4], in_=lams[:, 0:2], func=AF.Exp), reads=[lams], writes=[lams])
    P.op("dve", lambda e: e.tensor_scalar(out=lams[:, 4:5], in0=lams[:, 3:4], scalar1=lams[:, 2:3], scalar2=-lam_init,
                                          op0=ALU.subtract, op1=ALU.add), reads=[lams], writes=[lams])
    gc0, _ = VEC_COLS[f"diff_gain{l}"]
    P.op("dve", lambda e: e.tensor_scalar(out=lams[:, 5:6], in0=c.vecs[:, gc0:gc0 + 1], scalar1=1.0 - lam_init,
                                          scalar2=None, op0=ALU.mult), reads=[c.vecs], writes=[lams])
    cnt = {"s": 0, "p": 0}
    O = [c.bank[3], c.bank[4]]
    den = [c.bank[5], c.bank[6]]
    epi = c.bank7
    for j in range(8):
        qs = slice(j * 512, (j + 1) * 512)
        if j + 1 < 8:
            load_q(j + 1)
        Qb = Qz[j % 2]
        for h in range(4):
            nkb = 4 * j + 4
            s1, s2 = [], []
            for kb in range(nkb):
                for m in range(2):
                    ps = c.bank[cnt["s"] % 3]
                    cnt["s"] += 1
                    p_ = pt[cnt["p"] % 4]
                    cnt["p"] += 1

                    def f1(ps=ps, p_=p_, kb=kb, m=m, h=h, j=j, Qb=Qb):
                        diag = kb >= 4 * j
                        P.op("pe", lambda e: e.matmul(ps[:, :], K[h][:, kb * 128:(kb + 1) * 128], Qb[:, h * 2 + m, :],
                                                      start=True, stop=not diag), reads=[K[h], Qb], writes=[ps])
                        if diag:
                            r = kb - 4 * j
                            P.op("pe", lambda e: e.matmul(ps[:, :], c.identb[:, :], c.cmask[:, r, :], start=False, stop=True),
                                 reads=[c.identb, c.cmask], writes=[ps])
                        P.op("act", lambda e: e.activation(out=p_[:, :], in_=ps[:, :], func=AF.Exp, scale=0.125),
                             reads=[ps], writes=[p_])

                    def f2(p_=p_, kb=kb, m=m, h=h, nkb=nkb):
                        P.op("pe", lambda e: e.matmul(O[m][:, :], V[:, kb, h * 128:(h + 1) * 128], p_[:, :],
                                                      start=(kb == 0), stop=(kb == nkb - 1)), reads=[V, p_], writes=[O[m]])
                        P.op("pe", lambda e: e.matmul(den[m][:, :], c.onesb[:, :], p_[:, :],
                                                      start=(kb == 0), stop=(kb == nkb - 1)), reads=[c.onesb, p_], writes=[den[m]])
                    s1.append(f1)
                    s2.append(f2)
            skew_emit([s1, s2], 2)
            for m in range(2):
                P.op("act", lambda e, m=m: e.activation(out=bcs[m][:, :], in_=den[m][:, :], func=AF.Ln),
                     reads=[den[m]], writes=[bcs[m]])
                P.op("act", lambda e, m=m: e.activation(out=bcs[m][:, :], in_=bcs[m][:, :], func=AF.Exp, scale=-1.0),
                     reads=[bcs[m]], writes=[bcs[m]])
                P.op("dve", lambda e, m=m: e.tensor_tensor(out=tt_[m][:, :], in0=O[m][:, :], in1=bcs[m][:, :],
                                                           op=ALU.mult), reads=[O[m], bcs[m]], writes=[tt_[m]])
            P.op("dve", lambda e: e.scalar_tensor_tensor(out=yb[:, :], in0=tt_[1][:, :], scalar=lams[:, 4:5],
                                                         in1=tt_[0][:, :], op0=ALU.mult, op1=ALU.add),
                 reads=[tt_[0], tt_[1], lams], writes=[yb])
            P.op("act", lambda e: e.activation(out=sq[:, :], in_=yb[:, :], func=AF.Square), reads=[yb], writes=[sq])
            P.op("pe", lambda e: e.matmul(epi[:, :], c.ones32[:, :], sq[:, :], start=True, stop=True),
                 reads=[c.ones32, sq], writes=[epi])
            P.op("act", lambda e: e.activation(out=rs[:, :], in_=epi[:, :], func=AF.Ln, bias=c.epsb[:, 1:2],
                                               scale=1.0 / 128), reads=[epi, c.epsb], writes=[rs])
            P.op("act", lambda e: e.activation(out=rs[:, :], in_=rs[:, :], func=AF.Exp, scale=-0.5),
                 reads=[rs], writes=[rs])
            s_ = st[(j * 4 + h) % 2]
            P.op("dve", lambda e, s_=s_: e.scalar_tensor_tensor(out=s_[:, :], in0=yb[:, :], scalar=lams[:, 5:6],
                                                                in1=rs[:, :], op0=ALU.mult, op1=ALU.mult),
                 reads=[yb, rs, lams], writes=[s_])
            P.dma("sp", c.yT[h][:, qs], s_[:, :], reads=[s_], writes=[c.yT])


def attn_BD(P, c, l, kind):
    A = c.arena
    A.reset()
    isB = kind == "B"
    pt = [A.alloc(f"pt{i}", [128, 512], BF16) for i in range(4)]
    bcs = A.alloc("bcs", [128, 512], F32)
    st = [A.alloc(f"st{i}", [128, 512], BF16) for i in range(2)]
    V = A.alloc("v", [128, 32, 4, 65], BF16)
    P.op("pool", lambda e: e.memset(V[:, :, :, :], 1.0), writes=[V])
    vcol = 512 if isB else 1024
    for h in range(4):
        src = c.vtm[:, vcol + h * 64:vcol + (h + 1) * 64].rearrange("(kb p) c -> p kb c", p=128)
        P.dma("sp", V[:, :, h, 0:64], src, reads=[c.vtm], writes=[V])
    if isB:
        Q = [A.alloc(f"bq{h}", [128, S], BF16) for h in range(4)]
        K = [A.alloc(f"bk{h}", [128, S], BF16) for h in range(4)]
        for h in range(4):
            hr = slice((h % 2) * 64, (h % 2) * 64 + 64)
            P.dma("sp", Q[h][0:64, :], c.fmq[G_BQ + h // 2][hr, :], reads=[c.fmq], writes=[Q[h]])
            P.dma("sp", K[h][0:64, :], c.fmq[G_BK + h // 2][hr, :], reads=[c.fmq], writes=[K[h]])
            P.dma("sp", Q[h][64:65, :], c.caug_s[h:h + 1, :], reads=[c.caug_s], writes=[Q[h]])
            P.dma("sp", Q[h][65:66, :], c.caug_s[4 + h:5 + h, :], reads=[c.caug_s], writes=[Q[h]])
            P.op("pool", lambda e, h=h: e.memset(K[h][64:66, :], 1.0), writes=[K[h]])
        chunk_y0 = 4

        def load_q(j):
            pass
    else:
        Kp = [A.alloc(f"dk{i}", [128, S], BF16) for i in range(2)]
        for i in range(2):
            P.dma("sp", Kp[i][:, :], c.fmq[G_DK + i], reads=[c.fmq], writes=[Kp[i]])
        chunk_y0 = 8
        Qz = [A.alloc(f"dqz{b}", [128, 4, 512], BF16) for b in range(2)]
        IQc = [A.alloc(f"iqc{b}", [128, 4, 512], BF16) for b in range(2)]
        IKz = [A.alloc(f"ikz{m}", [128, S], BF16) for m in range(2)]
        for b in range(2):
            P.op("pool", lambda e, b=b: e.memset(Qz[b][:, :, :], 0.0), writes=[Qz[b]])
        for m in range(2):
            P.op("pool", lambda e, m=m: e.memset(IKz[m][:, :], 0.0), writes=[IKz[m]])
            P.dma("sp", IKz[m][m * 64:(m + 1) * 64, :], c.fmq[G_IK][0:64, :], reads=[c.fmq], writes=[IKz[m]])
        scs = [A.alloc(f"sc{i}", [128, S], F32) for i in range(2)]
        mbs = [A.alloc(f"mb{i}", [128, S], BF16) for i in range(4)]
        maskT = A.alloc("maskT", [128, 32, 512], BF16)
        rl = [A.alloc(f"rl{i}", [128, 512], F32) for i in range(2)]
        iw = A.alloc("iw", [128, 32, 8], F32)
        bss = [A.alloc(f"bs{i}", [128, 8 + 2 * NIT], F32) for i in range(2)]
        P.dma("sp", iw[:, :, :], c.iw_s[:, :].rearrange("(qb p) c -> p qb c", p=128), reads=[c.iw_s], writes=[iw])
        pw0, _ = VEC_COLS["pw"]

        def load_q(j):
            qs = slice(j * 512, (j + 1) * 512)
            for h in range(4):
                rows = slice((h % 2) * 64, (h % 2) * 64 + 64)
                P.dma("sp", Qz[j % 2][rows, h, :], c.fmq[G_DQ + h // 2][rows, qs], reads=[c.fmq], writes=[Qz[j % 2]])
            for g in range(4):
                P.dma("sp", IQc[j % 2][:, g, :], c.fmq[G_IQ + g][:, qs], reads=[c.fmq], writes=[IQc[j % 2]])
    cnt = {"s": 0, "p": 0, "acc": 0, "rl": 0}
    epi = c.bank[6]

    def idx_qb(j, sub, stage):
        qb = 4 * j + sub
        nval = (qb + 1) * 128
        mb = mbs[sub]
        IQb = IQc[j % 2]
        sc = scs[sub % 2]
        bs = bss[sub % 2]
        on_act = (sub % 2 == 1)
        cb0, _ = VEC_COLS["cntb"]
        if stage == "scores":
            for kc in range(j + 1):
                w = 512 if kc < j else (sub + 1) * 128
                ks = slice(kc * 512, kc * 512 + w)
                for hi_ in range(8):
                    ps = c.bank[cnt["s"] % 3]
                    cnt["s"] += 1
                    r = rl[cnt["rl"] % 2]
                    cnt["rl"] += 1
                    P.op("pe", lambda e, ps=ps, hi_=hi_, ks=ks, w=w: e.matmul(
                        ps[:, 0:w], IQb[:, hi_ // 2, sub * 128:(sub + 1) * 128], IKz[hi_ % 2][:, ks], start=True, stop=True),
                        reads=[IQb, IKz[hi_ % 2]], writes=[ps])
                    P.op("act", lambda e, ps=ps, r=r, w=w: e.activation(out=r[:, 0:w], in_=ps[:, 0:w], func=AF.Relu),
                         reads=[ps], writes=[r])
                    if hi_ == 0:
                        P.op("dve", lambda e, r=r, ks=ks, w=w: e.tensor_scalar(
                            out=sc[:, ks], in0=r[:, 0:w], scalar1=iw[:, qb, 0:1], scalar2=None, op0=ALU.mult),
                            reads=[r, iw], writes=[sc])
                    else:
                        P.op("dve", lambda e, r=r, ks=ks, w=w, hi_=hi_: e.scalar_tensor_tensor(
                            out=sc[:, ks], in0=r[:, 0:w], scalar=iw[:, qb, hi_:hi_ + 1], in1=sc[:, ks],
                            op0=ALU.mult, op1=ALU.add), reads=[r, iw, sc], writes=[sc])
            if qb >= 2:
                P.op("dve", lambda e: e.tensor_reduce(out=bs[:, 0:1], in_=sc[:, 0:nval], axis=mybir.AxisListType.X,
                                                      op=ALU.max), reads=[sc], writes=[bs])
                P.op("dve", lambda e: e.tensor_reduce(out=bs[:, 1:2], in_=sc[:, 0:nval], axis=mybir.AxisListType.X,
                                                      op=ALU.min), reads=[sc], writes=[bs])
            P.op("dve", lambda e: e.tensor_tensor(out=sc[:, qb * 128:(qb + 1) * 128], in0=sc[:, qb * 128:(qb + 1) * 128],
                                                  in1=c.cqk[:, :], op=ALU.add), reads=[sc, c.cqk], writes=[sc])
            if qb >= 2:
                P.op("dve", lambda e: e.tensor_tensor(out=bs[:, 2:3], in0=bs[:, 0:1], in1=bs[:, 1:2], op=ALU.subtract),
                     reads=[bs], writes=[bs])
                P.op("dve", lambda e: e.tensor_scalar(out=bs[:, 8:8 + NIT], in0=c.vecs[:, pw0:pw0 + NIT],
                                                      scalar1=bs[:, 2:3], scalar2=None, op0=ALU.mult),
                     reads=[bs, c.vecs], writes=[bs])
                P.op("dve", lambda e: e.scalar_tensor_tensor(out=bs[:, 3:4], in0=bs[:, 2:3], scalar=0.5, in1=bs[:, 1:2],
                                                             op0=ALU.mult, op1=ALU.add), reads=[bs], writes=[bs])
                if on_act:
                    P.op("dve", lambda e: e.tensor_scalar(out=bs[:, 8 + NIT:8 + 2 * NIT], in0=bs[:, 8:8 + NIT],
                                                          scalar1=-0.5, scalar2=None, op0=ALU.mult), reads=[bs], writes=[bs])
                    P.op("dve", lambda e: e.tensor_scalar(out=bs[:, 7:8], in0=bs[:, 3:4], scalar1=-1.0, scalar2=None,
                                                          op0=ALU.mult), reads=[bs], writes=[bs])
        elif stage == "bisect":
            if qb < 2:
                return
            if not on_act:
                for it in range(NIT):
                    P.op("dve", lambda e: e.tensor_scalar(
                        out=mb[:, 0:nval], in0=sc[:, 0:nval], scalar1=bs[:, 3:4], scalar2=None,
                        op0=ALU.is_ge, op1=ALU.add, accum_out=bs[:, 4:5]), reads=[sc, bs], writes=[mb, bs])
                    P.op("dve", lambda e: e.tensor_scalar(out=bs[:, 5:6], in0=bs[:, 4:5], scalar1=256.0, scalar2=-0.5,
                                                          op0=ALU.is_ge, op1=ALU.add), reads=[bs], writes=[bs])
                    if it < NIT - 1:
                        P.op("dve", lambda e, it=it: e.scalar_tensor_tensor(
                            out=bs[:, 3:4], in0=bs[:, 5:6], scalar=bs[:, 8 + it:9 + it], in1=bs[:, 3:4],
                            op0=ALU.mult, op1=ALU.add), reads=[bs], writes=[bs])
            else:
                for it in range(NIT):
                    P.op("act", lambda e: e.activation(out=mb[:, 0:nval], in_=sc[:, 0:nval], func=AF.Sign,
                                                       bias=bs[:, 7:8], accum_out=bs[:, 4:5]),
                         reads=[sc, bs], writes=[mb, bs])
                    P.op("act", lambda e: e.activation(out=bs[:, 5:6], in_=bs[:, 4:5], func=AF.Sign,
                                                       bias=c.vecs[:, cb0 + qb:cb0 + qb + 1]),
                         reads=[bs, c.vecs], writes=[bs])
                    if it < NIT - 1:
                        P.op("act", lambda e, it=it: e.activation(
                            out=bs[:, 7:8], in_=bs[:, 5:6], func=AF.Identity,
                            scale=bs[:, 8 + NIT + it:9 + NIT + it], bias=bs[:, 7:8]), reads=[bs], writes=[bs])
        else:
            if qb >= 2 and not on_act:
                P.op("dve", lambda e: e.tensor_scalar(out=bs[:, 5:6], in0=bs[:, 5:6], scalar1=-0.5, scalar2=None,
                                                      op0=ALU.add), reads=[bs], writes=[bs])
                P.op("dve", lambda e: e.scalar_tensor_tensor(
                    out=bs[:, 6:7], in0=bs[:, 5:6], scalar=bs[:, 8 + NIT - 1:8 + NIT], in1=bs[:, 3:4],
                    op0=ALU.mult, op1=ALU.add), reads=[bs], writes=[bs])
            elif qb >= 2:
                P.op("dve", lambda e: e.tensor_scalar(out=bs[:, 5:6], in0=bs[:, 5:6], scalar1=-1.0, scalar2=0.5,
                                                      op0=ALU.add, op1=ALU.mult), reads=[bs], writes=[bs])
                P.op("dve", lambda e: e.tensor_scalar(out=bs[:, 3:4], in0=bs[:, 7:8], scalar1=-1.0, scalar2=None,
                                                      op0=ALU.mult), reads=[bs], writes=[bs])
                P.op("dve", lambda e: e.scalar_tensor_tensor(
                    out=bs[:, 6:7], in0=bs[:, 5:6], scalar=bs[:, 8 + NIT - 1:8 + NIT], in1=bs[:, 3:4],
                    op0=ALU.mult, op1=ALU.add), reads=[bs], writes=[bs])
            else:
                P.op("dve", lambda e: e.memset(bs[:, 6:7], -1e29), writes=[bs])
            P.op("dve", lambda e: e.tensor_scalar(
                out=mb[:, 0:nval], in0=sc[:, 0:nval], scalar1=bs[:, 6:7], scalar2=NEG,
                op0=ALU.is_lt, op1=ALU.mult), reads=[sc, bs], writes=[mb])

    def idx_pair(j, s0):
        for sub in (s0, s0 + 1):
            idx_qb(j, sub, "scores")
        for sub in (s0, s0 + 1):
            idx_qb(j, sub, "bisect")
        for sub in (s0, s0 + 1):
            idx_qb(j, sub, "post")

    def transposes(j):
        for r_ in range(1, 4):
            kb = 4 * j + r_
            P.op("pool", lambda e, kb=kb, r_=r_: e.memset(maskT[:, kb, 0:128 * r_], NEG), writes=[maskT])
        for sub in range(4):
            qb = 4 * j + sub
            mb = mbs[sub]
            for kb0 in range(0, qb + 1, 4):
                nkk = min(4, qb + 1 - kb0)
                for kk in range(nkk):
                    kb = kb0 + kk
                    P.op("pe", lambda e, kb=kb, kk=kk, mb=mb: e.transpose(
                        c.bankT[:, kk * 128:(kk + 1) * 128], mb[:, kb * 128:(kb + 1) * 128], c.identb[:, :]),
                        reads=[mb, c.identb], writes=[c.bankT])
                P.op("act", lambda e, kb0=kb0, nkk=nkk, sub=sub: e.activation(
                    out=maskT[:, kb0:kb0 + nkk, sub * 128:(sub + 1) * 128],
                    in_=c.bankT[:, 0:nkk * 128].rearrange("p (a b) -> p a b", a=nkk), func=AF.Copy),
                    reads=[c.bankT], writes=[maskT])

    def attn_head(j, h):
        qs = slice(j * 512, (j + 1) * 512)
        po = c.bank[3 + (cnt["acc"] % 2)]
        cnt["acc"] += 1
        nkb = 4 * j + 4
        s1, s2 = [], []
        for kb in range(nkb):
            ps = c.bank[cnt["s"] % 3]
            cnt["s"] += 1
            p_ = pt[cnt["p"] % 4]
            cnt["p"] += 1

            def f1(ps=ps, p_=p_, kb=kb):
                diag = kb >= 4 * j
                masked = diag if isB else True
                if isB:
                    P.op("pe", lambda e: e.matmul(ps[:, :], K[h][0:66, kb * 128:(kb + 1) * 128], Q[h][0:66, qs],
                                                  start=True, stop=not masked), reads=[K[h], Q[h]], writes=[ps])
                else:
                    P.op("pe", lambda e: e.matmul(ps[:, :], Kp[h // 2][:, kb * 128:(kb + 1) * 128], Qz[j % 2][:, h, :],
                                                  start=True, stop=False), reads=[Kp[h // 2], Qz[j % 2]], writes=[ps])
                if masked:
                    if isB:
                        r = kb - 4 * j
                        P.op("pe", lambda e: e.matmul(ps[:, :], c.identb[:, :], c.cmask[:, r, :], start=False, stop=True),
                             reads=[c.identb, c.cmask], writes=[ps])
                    else:
                        P.op("pe", lambda e: e.matmul(ps[:, :], c.identb[:, :], maskT[:, kb, :], start=False, stop=True),
                             reads=[c.identb, maskT], writes=[ps])
                if isB:
                    P.op("act", lambda e: e.activation(out=p_[:, :], in_=ps[:, :], func=AF.Exp, scale=0.125,
                                                       bias=c.ncum[:, h, kb:kb + 1]), reads=[ps, c.ncum], writes=[p_])
                else:
                    P.op("act", lambda e: e.activation(out=p_[:, :], in_=ps[:, :], func=AF.Exp, scale=0.125),
                         reads=[ps], writes=[p_])

            def f2(p_=p_, kb=kb):
                P.op("pe", lambda e: e.matmul(po[0:65, :], V[:, kb, h, :], p_[:, :], start=(kb == 0), stop=(kb == nkb - 1)),
                     reads=[V, p_], writes=[po])
            s1.append(f1)
            s2.append(f2)
        skew_emit([s1, s2], 2)
        P.op("act", lambda e: e.activation(out=bcs[64:65, :], in_=po[64:65, :], func=AF.Ln), reads=[po], writes=[bcs])
        P.op("act", lambda e: e.activation(out=bcs[64:65, :], in_=bcs[64:65, :], func=AF.Exp, scale=-1.0),
             reads=[bcs], writes=[bcs])
        P.op("pe", lambda e: e.matmul(epi[0:64, :], c.ones32[64:65, 0:64], bcs[64:65, :], start=True, stop=True),
             reads=[c.ones32, bcs], writes=[epi])
        P.op("act", lambda e: e.activation(out=bcs[0:64, :], in_=epi[0:64, :], func=AF.Copy), reads=[epi], writes=[bcs])
        s_ = st[(j * 4 + h) % 2]
        P.op("dve", lambda e: e.tensor_tensor(out=s_[0:64, :], in0=po[0:64, :], in1=bcs[0:64, :], op=ALU.mult),
             reads=[po, bcs], writes=[s_])
        P.dma("sp", c.yT[chunk_y0 + h // 2][(h % 2) * 64:(h % 2) * 64 + 64, qs], s_[0:64, :], reads=[s_], writes=[c.yT])

    if isB:
        for j in range(8):
            for h in range(4):
                attn_head(j, h)
    else:
        load_q(0)
        idx_pair(0, 0)
        idx_pair(0, 2)
        transposes(0)
        for j in range(8):
            if j + 1 < 8:
                load_q(j + 1)
            for half in range(2):
                if j + 1 < 8:
                    idx_pair(j + 1, 2 * half)
                attn_head(j, 2 * half)
                attn_head(j, 2 * half + 1)
            if j + 1 < 8:
                transposes(j + 1)


def attn_C(P, c, l):
    A = c.arena
    A.reset()
    Qz = [A.alloc(f"cqz{b}", [128, 4, 512], BF16) for b in range(2)]
    Kp = [A.alloc(f"ck{i}", [128, S], BF16) for i in range(2)]
    V = A.alloc("cv", [128, 32, 256], BF16)
    E = [A.alloc(f"cE{i}", [128, 512], F32) for i in range(2)]
    L = [A.alloc(f"cL{i}", [128, 512], BF16) for i in range(4)]
    R = [A.alloc(f"cR{i}", [128, 512], BF16) for i in range(2)]
    pt = [A.alloc(f"pt{i}", [128, 512], BF16) for i in range(4)]
    st = [A.alloc(f"st{i}", [128, 512], BF16) for i in range(2)]
    for b in range(2):
        P.op("pool", lambda e, b=b: e.memset(Qz[b][:, :, :], 0.0), writes=[Qz[b]])

    def load_q(j):
        qs = slice(j * 512, (j + 1) * 512)
        for h in range(4):
            rows = slice((h % 2) * 64, (h % 2) * 64 + 64)
            P.dma("sp", Qz[j % 2][rows, h, :], c.fmq[G_CQ + h // 2][rows, qs], reads=[c.fmq], writes=[Qz[j % 2]])

    load_q(0)
    for i in range(2):
        P.dma("sp", Kp[i][:, :], c.fmq[G_CK + i], reads=[c.fmq], writes=[Kp[i]])
    P.dma("sp", V[:, :, :], c.vtm[:, 768:1024].rearrange("(kb p) c -> p kb c", p=128), reads=[c.vtm], writes=[V])
    cnt = {"s": 0, "n": 0, "acc": 0}
    for j in range(8):
        qs = slice(j * 512, (j + 1) * 512)
        if j + 1 < 8:
            load_q(j + 1)
        for h in range(4):
            Kh = Kp[h // 2]
            Qb = Qz[j % 2]
            po = c.bank[3 + (cnt["acc"] % 2)]
            Rb = R[cnt["acc"] % 2]
            cnt["acc"] += 1
            P.op("pool", lambda e, Rb=Rb: e.memset(Rb[:, :], 0.0), writes=[Rb])
            nkb = 4 * j + 4
            s1, s2, s3 = [], [], []
            for idx, kb in enumerate(range(nkb - 1, -1, -1)):
                ps = c.bank[cnt["s"] % 3]
                cnt["s"] += 1
                n = cnt["n"]
                cnt["n"] += 1
                Eb, Lb, p_ = E[n % 2], L[n % 4], pt[n % 4]

                def f1(ps=ps, Eb=Eb, Lb=Lb, kb=kb, h=h, j=j, Kh=Kh, Qb=Qb):
                    diag = kb >= 4 * j
                    r = kb - 4 * j
                    P.op("pe", lambda e: e.matmul(ps[:, :], Kh[:, kb * 128:(kb + 1) * 128], Qb[:, h, :], start=True, stop=False),
                         reads=[Kh, Qb], writes=[ps])
                    P.op("act", lambda e: e.activation(out=Eb[:, :], in_=ps[:, :], func=AF.Exp, scale=0.125),
                         reads=[ps], writes=[Eb])
                    P.op("act", lambda e: e.activation(out=Lb[:, :], in_=Eb[:, :], func=AF.Ln, bias=c.epsb[:, 2:3]),
                         reads=[Eb, c.epsb], writes=[Lb])
                    if diag:
                        P.op("dve", lambda e: e.tensor_tensor(out=Lb[:, :], in0=Lb[:, :], in1=c.cmask[:, 8 + r, :], op=ALU.mult),
                             reads=[Lb, c.cmask], writes=[Lb])

                def f2(ps=ps, Lb=Lb, p_=p_, kb=kb, j=j, Rb=Rb):
                    diag = kb >= 4 * j
                    r = kb - 4 * j
                    P.op("pe", lambda e: e.matmul(ps[:, :], c.trim8[:, :], Lb[:, :], start=False, stop=False, skip_group_check=True),
                         reads=[c.trim8, Lb], writes=[ps])
                    P.op("pe", lambda e: e.matmul(ps[:, :], c.onesm8[:, :], Rb[:, :], start=False, stop=not diag,
                                                  skip_group_check=True), reads=[c.onesm8, Rb], writes=[ps])
                    if diag:
                        P.op("pe", lambda e: e.matmul(ps[:, :], c.identb[:, :], c.cmask[:, 4 + r, :], start=False, stop=True,
                                                      skip_group_check=True), reads=[c.identb, c.cmask], writes=[ps])
                    P.op("act", lambda e: e.activation(out=p_[:, :], in_=ps[:, :], func=AF.Exp, scale=0.125),
                         reads=[ps], writes=[p_])
                    P.op("pool", lambda e: e.tensor_tensor(out=Rb[:, :], in0=Rb[:, :], in1=Lb[:, :], op=ALU.add),
                         reads=[Rb, Lb], writes=[Rb])

                def f3(p_=p_, kb=kb, h=h, idx=idx, nkb=nkb, po=po):
                    P.op("pe", lambda e: e.matmul(po[0:64, :], V[:, kb, h * 64:(h + 1) * 64], p_[:, :],
                                                  start=(idx == 0), stop=(idx == nkb - 1)), reads=[V, p_], writes=[po])
                s1.append(f1)
                s2.append(f2)
                s3.append(f3)
            skew_emit([s1, s2, s3], 1)
            s_ = st[(j * 4 + h) % 2]
            P.op("act", lambda e, po=po, s_=s_: e.activation(out=s_[0:64, :], in_=po[0:64, :], func=AF.Copy),
                 reads=[po], writes=[s_])
            P.dma("sp", c.yT[6 + h // 2][(h % 2) * 64:(h % 2) * 64 + 64, qs], s_[0:64, :], reads=[s_], writes=[c.yT])


def forget_prepass(P, c, l):
    A = c.arena
    A.reset()
    f = A.alloc("fT", [4, S], F32)
    t = A.alloc("ftmp", [4, S], F32)
    one = A.alloc("fone", [4, S], F32)
    hi = A.alloc("fhi", [4, S], BF16)
    lo = A.alloc("flo", [4, S], BF16)
    nb = A.alloc("fnb", [4, 1], F32)
    bc0, _ = VEC_COLS[f"b_fgt{l}"]
    P.dma("sp", f[:, :], c.fT_s[0:4, :], reads=[c.fT_s], writes=[f])
    P.op("dve", lambda e: e.tensor_scalar(out=nb[:, :], in0=c.vecs[0:4, bc0:bc0 + 1], scalar1=-1.0, scalar2=None,
                                          op0=ALU.mult), reads=[c.vecs], writes=[nb])
    P.op("act", lambda e: e.activation(out=t[:, :], in_=f[:, :], func=AF.Exp, scale=-1.0, bias=nb[:, 0:1]),
         reads=[f, nb], writes=[t])
    P.op("act", lambda e: e.activation(out=t[:, :], in_=t[:, :], func=AF.Ln, bias=c.epsb[0:4, 2:3]),
         reads=[t, c.epsb], writes=[t])
    P.op("dve", lambda e: e.tensor_scalar(out=t[:, :], in0=t[:, :], scalar1=-1.0, scalar2=None, op0=ALU.mult),
         reads=[t], writes=[t])
    P.op("dve", lambda e: e.memset(one[:, :], 1.0), writes=[one])
    P.op("dve", lambda e: e.tensor_tensor_scan(out=f[:, :], data0=one[:, :], data1=t[:, :], initial=0.0,
                                               op0=ALU.mult, op1=ALU.add), reads=[one, t], writes=[f])
    pT = c.bank[0]
    for kb in range(32):
        P.op("pe", lambda e, kb=kb: e.transpose(pT[:, kb * 4:(kb + 1) * 4], f[0:4, kb * 128:(kb + 1) * 128],
                                                c.ident[0:4, 0:4]), reads=[f, c.ident], writes=[pT])
    P.op("dve", lambda e: e.tensor_scalar(out=c.ncum[:, :, :], in0=pT[:, 0:128].rearrange("p (kb h) -> p h kb", h=4),
                                          scalar1=-1.0, scalar2=None, op0=ALU.mult), reads=[pT], writes=[c.ncum])
    P.op("dve", lambda e: e.tensor_scalar(out=hi[:, :], in0=f[:, :], scalar1=8.0, scalar2=None, op0=ALU.mult),
         reads=[f], writes=[hi])
    P.op("dve", lambda e: e.scalar_tensor_tensor(out=lo[:, :], in0=f[:, :], scalar=8.0, in1=hi[:, :],
                                                 op0=ALU.mult, op1=ALU.subtract), reads=[f, hi], writes=[lo])
    P.dma("sp", c.caug_s[0:4, :], hi[:, :], reads=[hi], writes=[c.caug_s])
    P.dma("sp", c.caug_s[4:8, :], lo[:, :], reads=[lo], writes=[c.caug_s])


def merge_phase(P, c, l, x_src, x_dst):
    A = c.arena
    A.reset()
    xt = A.alloc("xt", [128, 4, D], F32)
    hT = A.alloc("hT", [128, 8, TT], BF16)
    yT = A.alloc("yTt", [128, 10, TT], BF16)
    mg = A.alloc("mg", [128, 4, D], F32)
    mT = A.alloc("mT", [128, 8, TT], BF16)
    sig = [A.alloc(f"sig{i}", [128, 512], F32) for i in range(2)]
    tmp = [A.alloc(f"mtmp{i}", [128, 512], F32) for i in range(2)]
    wg = A.alloc("wg", [128, 8, 4096], BF16)
    wb = A.alloc("wb", [128, 10, D], BF16)
    wo = A.alloc("wo", [128, 8, D], BF16)
    bg = A.alloc("bg", [1, 4096], BF16, parts=1)
    for i in range(4):
        P.dma("sp", wg[:, 2 * i:2 * i + 2, :], c.wbf[f"wgate{l}"][:, 2 * i:2 * i + 2, :], reads=[c.wbf[f"wgate{l}"]], writes=[wg])
    P.dma("sp", wb[:, :, :], c.wbf[f"wbr{l}"][:, :, :], reads=[c.wbf[f"wbr{l}"]], writes=[wb])
    P.dma("sp", wo[:, :, :], c.wbf[f"wout{l}"][:, :, :], reads=[c.wbf[f"wout{l}"]], writes=[wo])
    P.dma("sp", bg[:, :], c.wbf[f"bgate{l}"][:, :], reads=[c.wbf[f"bgate{l}"]], writes=[bg])
    chunks = [(0, 4), (4, 2), (6, 2), (8, 2)]
    n = 0
    for tt in range(NTT):
        tok = slice(tt * TT, (tt + 1) * TT)
        P.dma("sp", xt[:, :, :], x_tile_ap(x_src, tt), reads=[x_src], writes=[xt])
        P.dma("sp", hT[:, :, :], c.hT_s[tt], reads=[c.hT_s], writes=[hT])
        P.dma("sp", yT[:, :, :], c.yT[:, :, tok].rearrange("c p t -> p c t"), reads=[c.yT], writes=[yT])
        for i in range(4):
            ch0, nch = chunks[i]
            for dh in range(2):
                gc = slice(i * 1024 + dh * 512, i * 1024 + dh * 512 + 512)
                for tb in range(4):
                    ts_ = slice(tb * 128, (tb + 1) * 128)
                    pg = c.bank[(n % 2) * 2]
                    pb = c.bank[(n % 2) * 2 + 1]
                    sg_, tm_ = sig[n % 2], tmp[n % 2]
                    n += 1
                    for kc in range(8):
                        P.op("pe", lambda e, pg=pg, kc=kc, ts_=ts_, gc=gc: e.matmul(
                            pg[:, :], hT[:, kc, ts_], wg[:, kc, gc], start=(kc == 0), stop=False),
                            reads=[hT, wg], writes=[pg])
                    P.op("pe", lambda e, pg=pg, gc=gc: e.matmul(pg[:, :], c.onesb[0:1, :], bg[0:1, gc], start=False, stop=True),
                         reads=[c.onesb, bg], writes=[pg])
                    P.op("act", lambda e, pg=pg, sg_=sg_: e.activation(out=sg_[:, :], in_=pg[:, :], func=AF.Sigmoid),
                         reads=[pg], writes=[sg_])
                    for k in range(nch):
                        P.op("pe", lambda e, pb=pb, k=k, ch0=ch0, nch=nch, ts_=ts_, dh=dh: e.matmul(
                            pb[:, :], yT[:, ch0 + k, ts_], wb[:, ch0 + k, dh * 512:(dh + 1) * 512],
                            start=(k == 0), stop=(k == nch - 1)), reads=[yT, wb], writes=[pb])
                    ms_ = mg[:, tb, dh * 512:(dh + 1) * 512]
                    if i == 0:
                        P.op("dve", lambda e, ms_=ms_, sg_=sg_, pb=pb: e.tensor_tensor(out=ms_, in0=sg_[:, :], in1=pb[:, :],
                                                                                       op=ALU.mult), reads=[sg_, pb], writes=[mg])
                    else:
                        P.op("dve", lambda e, tm_=tm_, sg_=sg_, pb=pb: e.tensor_tensor(out=tm_[:, :], in0=sg_[:, :], in1=pb[:, :],
                                                                                       op=ALU.mult), reads=[sg_, pb], writes=[tm_])
                        P.op("pool", lambda e, ms_=ms_, tm_=tm_: e.tensor_tensor(out=ms_, in0=ms_, in1=tm_[:, :], op=ALU.add),
                             reads=[mg, tm_], writes=[mg])
        transpose_to_fm(P, c, mg, mT, None)
        for tb in range(4):
            for dh in range(2):
                po = c.bank[4 + (n % 2)]
                n += 1
                for kc in range(8):
                    P.op("pe", lambda e, po=po, kc=kc, tb=tb, dh=dh: e.matmul(
                        po[:, :], mT[:, kc, tb * 128:(tb + 1) * 128], wo[:, kc, dh * 512:(dh + 1) * 512],
                        start=(kc == 0), stop=(kc == 7)), reads=[mT, wo], writes=[po])
                P.op("dve", lambda e, po=po, tb=tb, dh=dh: e.tensor_tensor(
                    out=xt[:, tb, dh * 512:(dh + 1) * 512], in0=po[:, :], in1=xt[:, tb, dh * 512:(dh + 1) * 512],
                    op=ALU.add), reads=[po, xt], writes=[xt])
        P.dma("sp", x_tile_ap(x_dst, tt), xt[:, :, :], reads=[xt], writes=[x_dst])


ARENA_BYTES = 188 * 1024


def build_program(phases=("all",), debug=()):
    nc = bass.Bass("TRN2", target_bir_lowering=False)
    P = Prog(nc)
    c = Ctx()
    kd = lambda n: ("ExternalOutput" if n in debug else "Internal")
    c.x_in = P.dram("x", [S, D], F32, kind="ExternalInput")
    c.out = P.dram("out", [S, D], F32, kind="ExternalOutput")
    c.pos_in = P.dram("positions", [S], I32, kind="ExternalInput")
    nv = sum(v[1] for v in VEC_COLS.values())
    c.vecs_in = P.dram("vecs", [128, nv], F32, kind="ExternalInput")
    c.ident_in = P.dram("ident", [128, 128], F32, kind="ExternalInput")
    c.cmask_in = P.dram("cmask", [12, 128, 512], F32, kind="ExternalInput")
    c.cmat_in = P.dram("cmat", [4, 128, 128], F32, kind="ExternalInput")
    c.cqk_in = P.dram("cqk", [128, 128], F32, kind="ExternalInput")
    c.gfin_in = P.dram("final_norm", [D], F32, kind="ExternalInput")
    c.lam_in = {nm: P.dram(nm, [DEPTH, 64], F32, kind="ExternalInput") for nm in ("lam_q1", "lam_k1", "lam_q2", "lam_k2")}
    c.xres = P.dram("xres", [S, D], F32, kind=kd("xres"))
    c.hT_s = P.dram("hT_s", [NTT, 128, 8, TT], BF16, kind=kd("hT_s"))
    c.fmq = P.dram("fmq", [NG, 128, S], BF16, kind=kd("fmq"))
    c.fT_s = P.dram("fT_s", [4, S], F32, kind=kd("fT_s"))
    c.vtm = P.dram("vtm", [S, 1280], BF16, kind=kd("vtm"))
    c.iw_s = P.dram("iw_s", [S, 8], F32, kind=kd("iw_s"))
    c.caug_s = P.dram("caug_s", [8, S], BF16, kind=kd("caug_s"))
    c.yT = P.dram("yT", [10, 128, S], BF16, kind=kd("yT"))
    c.ropeC_s = P.dram("ropeC_s", [128, S], F32, kind=kd("ropeC_s"))
    c.ropeS_s = P.dram("ropeS_s", [128, S], F32, kind=kd("ropeS_s"))
    c.w32, c.wbf = {}, {}
    for name, shape in weight_specs():
        c.w32[name] = P.dram(name, shape, F32, kind="ExternalInput")
        c.wbf[name] = P.dram(name + "_bf", shape, BF16, kind="Internal")
        c.wbf[name].res.no_barrier = True

    c.vecs = P.sb("vecs_sb", [128, nv], F32)
    c.ident = P.sb("ident_sb", [128, 128], F32)
    c.ones32 = P.sb("ones32", [128, 128], F32)
    c.cqk = P.sb("cqk_sb", [128, 128], F32)
    c.epsb = P.sb("epsb", [128, 4], F32)
    c.ms = P.sb("ms", [128, 12], F32)
    c.ncum = P.sb("ncum", [128, 4, 32], F32)
    c.cmask = P.sb("cmask_sb", [128, 12, 512], BF16)
    c.cmatb = P.sb("cmat_sb", [128, 4, 128], BF16)
    arena_t = nc.alloc_sbuf_tensor("arena", [128, ARENA_BYTES // 2], BF16)
    c.arena = Arena(P, arena_t, ARENA_BYTES)
    c.bank = [P.ps(f"bank{i}", [128, 512]) for i in range(7)]
    c.bankT = P.ps("bankT", [128, 1024], BF16)
    c.bank7 = Buf(c.bankT.t[:, :].bitcast(F32), "bank7")
    c.bank7.res = c.bankT.res

    class _V:
        pass
    mk = lambda i: type("B", (), {"t": c.cmatb.t, "res": c.cmatb.res, "__getitem__": lambda s, idx, i=i: c.cmatb.t[:, i, :][idx]})()
    c.identb, c.trim8, c.onesm8, c.onesb = mk(0), mk(1), mk(2), mk(3)

    P.dma("sp", c.vecs[:, :], c.vecs_in[:, :], reads=[c.vecs_in], writes=[c.vecs])
    P.dma("sp", c.ident[:, :], c.ident_in[:, :], reads=[c.ident_in], writes=[c.ident])
    P.dma("sp", c.cqk[:, :], c.cqk_in[:, :], reads=[c.cqk_in], writes=[c.cqk])
    P.dma("pool", c.cmask[:, :, :], c.cmask_in.t.rearrange("r p q -> p r q"), reads=[c.cmask_in], writes=[c.cmask])
    P.dma("pool", c.cmatb[:, :, :], c.cmat_in.t.rearrange("r p q -> p r q"), reads=[c.cmat_in], writes=[c.cmatb])
    P.op("dve", lambda e: e.memset(c.ones32[:, :], 1.0), writes=[c.ones32])
    P.op("dve", lambda e: e.memset(c.epsb[:, 0:1], 1e-6), writes=[c.epsb])
    P.op("dve", lambda e: e.memset(c.epsb[:, 1:2], 1e-5), writes=[c.epsb])
    P.op("dve", lambda e: e.memset(c.epsb[:, 2:3], 1.0), writes=[c.epsb])

    for name, shape in weight_specs():
        n = int(np.prod(shape))
        assert n % 2048 == 0, name
        rows = n // 2048
        src, dst = c.w32[name], c.wbf[name]
        pat = " ".join(f"a{i}" for i in range(len(shape)))
        s2 = src.t.rearrange(f"{pat} -> ({pat})").rearrange("(r c) -> r c", c=2048)
        d2 = dst.t.rearrange(f"{pat} -> ({pat})").rearrange("(r c) -> r c", c=2048)
        r0 = 0
        while r0 < rows:
            r1 = min(rows, r0 + 128)
            P.dma("pool", d2[r0:r1, :], s2[r0:r1, :], reads=[src], writes=[dst])
            r0 = r1

    ALL = "all" in phases
    on = lambda nm: ALL or nm in phases
    if on("rope"):
        rope_tables(P, c)
    cur = c.x_in
    for l in range(DEPTH):
        if on(f"ffn1_{l}"):
            ffn_phase(P, c, l, "ffn1", cur, c.xres)
            cur = c.xres
        if on(f"m1_{l}"):
            mix_proj_phase(P, c, l, cur)
        if on(f"fgt_{l}"):
            forget_prepass(P, c, l)
        if on(f"A_{l}"):
            attn_A(P, c, l)
        if on(f"B_{l}"):
            attn_BD(P, c, l, "B")
        if on(f"C_{l}"):
            attn_C(P, c, l)
        if on(f"D_{l}"):
            attn_BD(P, c, l, "D")
        if on(f"m3_{l}"):
            merge_phase(P, c, l, cur, c.xres)
            cur = c.xres
        if on(f"ffn2_{l}"):
            ffn_phase(P, c, l, "ffn2", cur, c.xres)
            cur = c.xres
    if on("final"):
        final_phase(P, c, cur, c.out)
    else:
        pass
    P.out_events += [(r.sem, r.semv) for r in P.dma_res]
    P.finish()
    return nc, P


def make_in_maps(inputs):
    x = np.asarray(inputs["x"], dtype=np.float32)
    vecs = build_vecs(inputs)
    hw = host_weights(inputs)
    cst = host_consts()
    maps = []
    for b in range(8):
        m = {"x": np.ascontiguousarray(x[b]),
             "positions": np.asarray(inputs["positions"], dtype=np.int32),
             "vecs": vecs, "final_norm": np.asarray(inputs["final_norm"], dtype=np.float32)}
        for nm in ("lam_q1", "lam_k1", "lam_q2", "lam_k2"):
            m[nm] = np.asarray(inputs[nm], dtype=np.float32)
        m.update(cst)
        m.update(hw)
        maps.append(m)
    return maps


def kernel(**inputs):
    maps = make_in_maps(inputs)
    nc, P = build_program()
    res = run_bass_kernel_spmd(nc, maps, core_ids=list(range(8)))
    return np.stack([res.results[b]["out"] for b in range(8)], axis=0).astype(np.float32)
```
